# Optimizing a Trainium2 kernel written in Bass

```python
import math
import jax, jax.numpy as jnp
from jax import lax
import numpy as np

D_MODEL = 1024
BATCH = 16
SEQ = 2048
DEPTH = 2

CHUNK = 64
N_MIXERS = 2
N_MLSTM = (DEPTH + 1) // 2
N_DIFF = DEPTH // 2
M_HEADS = 4
M_DK = D_MODEL // M_HEADS
M_DV = D_MODEL // M_HEADS
M_CONV = 4
M_IN_COLS = 2 * M_HEADS * M_DK + 2 * M_HEADS * M_DV + 2 * M_HEADS
D_HEADS = 8
D_HD = D_MODEL // (2 * D_HEADS)
D_VD = 2 * D_HD
D_IN_COLS = 3 * D_HEADS * D_VD
Q_BLOCK = 128
REL_BUCKETS = 32
REL_MAX_DIST = 128
D_FF = 2816
FFN_CONV = 3
LN_EPS = 1e-5
DEEPNORM_ALPHA = (2.0 * DEPTH) ** 0.25
DEEPNORM_BETA = (8.0 * DEPTH) ** -0.25

kernel_name = "hybrid_mlstm_diffattn_convffn_trunk"


def _layer_norm(x, g, b):
    xf = x.astype(jnp.float32)
    mu = jnp.mean(xf, axis=-1, keepdims=True)
    var = jnp.mean(jnp.square(xf - mu), axis=-1, keepdims=True)
    y = (xf - mu) * lax.rsqrt(var + LN_EPS) * g.astype(jnp.float32) + b.astype(jnp.float32)
    return y.astype(x.dtype)


def _rms_heads(h, g, n_heads):
    B, S, _ = h.shape
    hh = h.reshape(B, S, n_heads, -1)
    hh = hh * lax.rsqrt(jnp.mean(hh * hh, axis=-1, keepdims=True) + LN_EPS)
    return hh.reshape(B, S, -1) * g.astype(jnp.float32)


def _causal_dwconv(x, w, b):
    W = w.shape[0]
    S = x.shape[1]
    xp = jnp.pad(x, ((0, 0), (W - 1, 0), (0, 0)))
    y = b
    for j in range(W):
        y = y + xp[:, j:j + S, :] * w[j]
    return y


def _t5_bucket(rel):
    nb = REL_BUCKETS // 2
    max_exact = nb // 2
    n = -rel
    ret = jnp.where(n < 0, nb, 0)
    n = jnp.abs(n)
    nf = jnp.maximum(n, 1).astype(jnp.float32)
    large = max_exact + (jnp.log(nf / max_exact) / math.log(REL_MAX_DIST / max_exact)
                         * (nb - max_exact)).astype(jnp.int32)
    large = jnp.minimum(large, nb - 1)
    return ret + jnp.where(n < max_exact, n, large)


def _lambda_init(layer_idx):
    return 0.8 - 0.6 * math.exp(-0.3 * layer_idx)


def _mlstm_mixer(u, w_in, b_gate, conv_w, conv_b, norm_g, w_out):
    B, S, _ = u.shape
    H, L = M_HEADS, CHUNK
    NC = S // L
    f32 = jnp.float32
    proj = u @ w_in
    s1 = 2 * H * M_DK
    s2 = s1 + H * M_DV
    s3 = s2 + H * M_DV
    qk, v, o, gates = jnp.split(proj, [s1, s2, s3], axis=-1)
    qk = jax.nn.silu(_causal_dwconv(qk, conv_w, conv_b))
    q, k = jnp.split(qk, 2, axis=-1)
    gates = (gates + b_gate).astype(f32)
    ig = gates[..., :H]
    lf = jax.nn.log_sigmoid(gates[..., H:])

    def to_chunks(t, d):
        return t.astype(f32).reshape(B, NC, L, H, d).transpose(1, 0, 3, 2, 4)

    qc = to_chunks(q, M_DK)
    kc = to_chunks(k, M_DK) * (M_DK ** -0.5)
    vc = to_chunks(v, M_DV)
    igc = ig.reshape(B, NC, L, H).transpose(1, 0, 3, 2)
    lfc = lf.reshape(B, NC, L, H).transpose(1, 0, 3, 2)
    tril = jnp.tril(jnp.ones((L, L), dtype=bool))

    def step(carry, inp):
        C, n, m = carry
        qb, kb, vb, ib, fb = inp
        bcum = jnp.cumsum(fb, axis=-1)
        dmat = jnp.where(tril, bcum[..., :, None] - bcum[..., None, :] + ib[..., None, :], -jnp.inf)
        inter = bcum + m[..., None]
        m_t = jnp.maximum(inter, jnp.max(dmat, axis=-1))
        w = jnp.exp(dmat - m_t[..., None])
        s_inter = jnp.exp(inter - m_t)
        a = jnp.einsum('bhtd,bhsd->bhts', qb, kb) * w
        num = s_inter[..., None] * jnp.einsum('bhtd,bhde->bhte', qb, C) + jnp.einsum('bhts,bhse->bhte', a, vb)
        den = s_inter * jnp.einsum('bhtd,bhd->bht', qb, n) + jnp.sum(a, axis=-1)
        h = num / jnp.maximum(jnp.abs(den), jnp.exp(-m_t))[..., None]
        b_last = bcum[..., -1]
        g = b_last[..., None] - bcum + ib
        m_new = jnp.maximum(b_last + m, jnp.max(g, axis=-1))
        decay = jnp.exp(b_last + m - m_new)
        w_s = jnp.exp(g - m_new[..., None])
        C_new = decay[..., None, None] * C + jnp.einsum('bhsd,bhse->bhde', kb * w_s[..., None], vb)
        n_new = decay[..., None] * n + jnp.einsum('bhs,bhsd->bhd', w_s, kb)
        return (C_new, n_new, m_new), h

    init = (jnp.zeros((B, H, M_DK, M_DV), f32), jnp.zeros((B, H, M_DK), f32), jnp.zeros((B, H), f32))
    _, hc = lax.scan(step, init, (qc, kc, vc, igc, lfc))
    h = hc.transpose(1, 0, 3, 2, 4).reshape(B, S, H * M_DV)
    h = _rms_heads(h, norm_g, H) * jax.nn.sigmoid(o.astype(f32))
    return h.astype(u.dtype) @ w_out


def _diff_attn_mixer(u, w_in, lam, norm_g, w_out, rel_bias, lambda_init):
    B, S, _ = u.shape
    H = D_HEADS
    f32 = jnp.float32
    proj = u @ w_in
    q, k, v = jnp.split(proj, 3, axis=-1)
    q = q.astype(f32).reshape(B, S, H, 2, D_HD).transpose(0, 2, 3, 1, 4) * (D_HD ** -0.5)
    k = k.astype(f32).reshape(B, S, H, 2, D_HD).transpose(0, 2, 3, 1, 4)
    v = v.astype(f32).reshape(B, S, H, D_VD).transpose(0, 2, 1, 3)
    lam = lam.astype(f32)
    lam_full = jnp.exp(jnp.sum(lam[0] * lam[1])) - jnp.exp(jnp.sum(lam[2] * lam[3])) + lambda_init
    table = rel_bias.astype(f32)
    kpos = jnp.arange(S)

    def block(j):
        q0 = j * Q_BLOCK
        qb = lax.dynamic_slice_in_dim(q, q0, Q_BLOCK, axis=3)
        qpos = q0 + jnp.arange(Q_BLOCK)
        bias = table[_t5_bucket(kpos[None, :] - qpos[:, None])].transpose(2, 0, 1)
        allowed = (kpos[None, :] // CHUNK) <= (qpos[:, None] // CHUNK)
        logits = jnp.einsum('bhcqd,bhckd->bhcqk', qb, k) + bias[None, :, None]
        logits = jnp.where(allowed, logits, -jnp.inf)
        p = jax.nn.softmax(logits, axis=-1)
        pdiff = p[:, :, 0] - lam_full * p[:, :, 1]
        return jnp.einsum('bhqk,bhkd->bhqd', pdiff, v)

    out = lax.map(block, jnp.arange(S // Q_BLOCK))
    out = out.transpose(1, 0, 3, 2, 4).reshape(B, S, H * D_VD)
    out = _rms_heads(out, norm_g, H) * (1.0 - lambda_init)
    return out.astype(u.dtype) @ w_out


def _conv_ffn(u, w_up, conv_w, conv_b, w_down):
    h = _causal_dwconv(u @ w_up, conv_w, conv_b)
    a, g = jnp.split(h, 2, axis=-1)
    return (jax.nn.gelu(g, approximate=False) * a) @ w_down


def setup_inputs(seed: int = 0) -> dict:
    key = jax.random.key(seed)
    ks = jax.random.split(key, 24)
    f32 = jnp.float32
    nrm = lambda k, shape, s: jax.random.normal(k, shape, f32) * s
    D = D_MODEL
    b_i = nrm(ks[7], (N_MLSTM, M_HEADS), 0.1)
    b_f = jnp.linspace(3.0, 6.0, M_HEADS, dtype=f32)[None, :] + nrm(ks[8], (N_MLSTM, M_HEADS), 0.1)
    return {
        "x": nrm(ks[0], (BATCH, SEQ, D), 1.0),
        "c": nrm(ks[1], (BATCH, D), 1.0),
        "w_ada": nrm(ks[2], (DEPTH, D, 6 * D), 0.1 * D ** -0.5),
        "b_ada": nrm(ks[3], (DEPTH, 6 * D), 0.02),
        "ln_g": 1.0 + nrm(ks[4], (DEPTH, 2, D), 0.02),
        "ln_b": nrm(ks[5], (DEPTH, 2, D), 0.02),
        "m_w_in": nrm(ks[6], (N_MLSTM, D, M_IN_COLS), D ** -0.5),
        "m_b_gate": jnp.concatenate([b_i, b_f], axis=-1),
        "m_conv_w": nrm(ks[9], (N_MLSTM, M_CONV, 2 * M_HEADS * M_DK), M_CONV ** -0.5),
        "m_conv_b": nrm(ks[10], (N_MLSTM, 2 * M_HEADS * M_DK), 0.02),
        "m_norm_g": 1.0 + nrm(ks[11], (N_MLSTM, M_HEADS * M_DV), 0.02),
        "m_w_out": nrm(ks[12], (N_MLSTM, M_HEADS * M_DV, D), DEEPNORM_BETA * (M_HEADS * M_DV) ** -0.5),
        "d_w_in": nrm(ks[13], (N_DIFF, D, D_IN_COLS), D ** -0.5),
        "d_lambda": nrm(ks[14], (N_DIFF, 4, D_HD), 0.1),
        "d_norm_g": 1.0 + nrm(ks[15], (N_DIFF, D_HEADS * D_VD), 0.02),
        "d_w_out": nrm(ks[16], (N_DIFF, D_HEADS * D_VD, D), DEEPNORM_BETA * (D_HEADS * D_VD) ** -0.5),
        "rel_bias": nrm(ks[17], (REL_BUCKETS, D_HEADS), 0.3),
        "f_w_up": nrm(ks[18], (DEPTH, D, 2 * D_FF), D ** -0.5),
        "f_conv_w": nrm(ks[19], (DEPTH, FFN_CONV, 2 * D_FF), FFN_CONV ** -0.5),
        "f_conv_b": nrm(ks[20], (DEPTH, 2 * D_FF), 0.02),
        "f_w_down": nrm(ks[21], (DEPTH, D_FF, D), DEEPNORM_BETA * D_FF ** -0.5),
    }


def reference(x, c, w_ada, b_ada, ln_g, ln_b, m_w_in, m_b_gate, m_conv_w, m_conv_b, m_norm_g, m_w_out,
              d_w_in, d_lambda, d_norm_g, d_w_out, rel_bias, f_w_up, f_conv_w, f_conv_b, f_w_down):
    cs = jax.nn.silu(c)
    for i in range(DEPTH):
        ada = cs @ w_ada[i] + b_ada[i]
        sh1, sc1, g1, sh2, sc2, g2 = [t[:, None, :] for t in jnp.split(ada, 6, axis=-1)]
        u = x * (1.0 + sc1) + sh1
        j = i // N_MIXERS
        if i % N_MIXERS == 0:
            y = _mlstm_mixer(u, m_w_in[j], m_b_gate[j], m_conv_w[j], m_conv_b[j], m_norm_g[j], m_w_out[j])
        else:
            y = _diff_attn_mixer(u, d_w_in[j], d_lambda[j], d_norm_g[j], d_w_out[j], rel_bias, _lambda_init(i))
        x = _layer_norm(DEEPNORM_ALPHA * x + (1.0 + g1) * y, ln_g[i, 0], ln_b[i, 0])
        u = x * (1.0 + sc2) + sh2
        y = _conv_ffn(u, f_w_up[i], f_conv_w[i], f_conv_b[i], f_w_down[i])
        x = _layer_norm(DEEPNORM_ALPHA * x + (1.0 + g2) * y, ln_g[i, 1], ln_b[i, 1])
    return x
```

```python
import math
import numpy as np
import concourse.bass as bass
import concourse.mybir as mybir
from concourse.bass_utils import run_bass_kernel_spmd

F32 = mybir.dt.float32
BF16 = mybir.dt.bfloat16
AF = mybir.ActivationFunctionType
ALU = mybir.AluOpType
AX = mybir.AxisListType

NCORES = 8
B_PER = 2
S = 2048
D = 1024
DFF = 2816
NG_FF = 22
ALPHA = (2.0 * 2) ** 0.25
LN_EPS = 1e-5
EPS_LN = LN_EPS / (ALPHA * ALPHA)
EPOCH = 20000


class Prog:
    def __init__(self, nc):
        self.nc = nc
        self.ops = []
        self.last_writer = {}
        self.readers = {}
        self.arena_names = set()

    def fence_arena(self):
        self.op('pool', lambda e: e.memset(self.fence_ap, 0.0), (), ["ARENA", "fence_ap"])

    def op(self, eng, fn, reads=(), writes=(), dma_key=None, final=False):
        reads = list(reads)
        writes = list(writes)
        for kk in reads + writes:
            nm = kk if isinstance(kk, str) else kk[0]
            if nm in self.arena_names:
                reads.append("ARENA")
                break
        deps = set()
        for k in reads:
            w = self.last_writer.get(k)
            if w is not None:
                deps.add(w)
            if not isinstance(k, str) and k[0] == "ps":
                deps.update(r for r in self.readers.get(k, ()) if self.ops[r]['eng'] != eng)
        for k in writes:
            w = self.last_writer.get(k)
            if w is not None:
                deps.add(w)
            deps.update(self.readers.get(k, ()))
        idx = len(self.ops)
        self.ops.append(dict(eng=eng, fn=fn, deps=deps, dma_key=dma_key, has_dep=(final or dma_key is not None), final=final))
        for k in reads:
            self.readers.setdefault(k, []).append(idx)
        for k in writes:
            self.last_writer[k] = idx
            self.readers[k] = []
        return idx

    def pe(self, fn, reads=(), writes=()):
        return self.op('pe', fn, reads, writes)

    def act(self, fn, reads=(), writes=()):
        return self.op('act', fn, reads, writes)

    def dve(self, fn, reads=(), writes=()):
        return self.op('dve', fn, reads, writes)

    def pool(self, fn, reads=(), writes=()):
        return self.op('pool', fn, reads, writes)

    def dma(self, out, in_, reads=(), writes=(), key=None, q='sp', final=False):
        assert key is not None
        return self.op(q, lambda e: e.dma_start(out=out, in_=in_), reads, writes, dma_key=key, final=final)

    def mm(self, out, lhsT, rhs, start, stop, reads, writes):
        return self.pe(lambda e: e.matmul(out, lhsT, rhs, start=start, stop=stop), reads, writes)

    def tr(self, out, in_, ident, reads, writes):
        return self.pe(lambda e: e.transpose(out, in_, ident), reads, writes)

    def actv(self, eng, out, in_, func, reads, writes, bias=None, scale=None, accum=None):
        kw = {}
        if bias is not None:
            kw['bias'] = bias
        if scale is not None:
            kw['scale'] = scale
        if accum is not None:
            kw['accum_out'] = accum
        return self.op(eng, lambda e: e.activation(out=out, in_=in_, func=func, **kw), reads, writes)

    def tt(self, eng, out, in0, in1, op, reads, writes):
        return self.op(eng, lambda e: e.tensor_tensor(out=out, in0=in0, in1=in1, op=op), reads, writes)

    def ts(self, eng, out, in0, s1, s2, op0, op1, reads, writes):
        if s2 is None:
            return self.op(eng, lambda e: e.tensor_scalar(out=out, in0=in0, scalar1=s1, scalar2=None, op0=op0),
                           reads, writes)
        return self.op(eng, lambda e: e.tensor_scalar(out=out, in0=in0, scalar1=s1, scalar2=s2, op0=op0, op1=op1),
                       reads, writes)

    def stt(self, eng, out, in0, scalar, in1, op0, op1, reads, writes):
        return self.op(eng, lambda e: e.scalar_tensor_tensor(out=out, in0=in0, scalar=scalar, in1=in1,
                                                             op0=op0, op1=op1), reads, writes)

    def copy(self, eng, out, in_, reads, writes):
        if eng == 'act':
            return self.op(eng, lambda e: e.copy(out=out, in_=in_), reads, writes)
        return self.op(eng, lambda e: e.tensor_copy(out=out, in_=in_), reads, writes)

    def memset(self, eng, ap, val, writes):
        return self.op(eng, lambda e: e.memset(ap, val), (), writes)

    def emit(self, final_keys=()):
        nc = self.nc
        ops = self.ops

        def skip(dop, o):
            return dop['eng'] == 'pe' and o['eng'] == 'pe' and dop['dma_key'] is None

        for o in ops:
            for d in o['deps']:
                if not skip(ops[d], o):
                    ops[d]['has_dep'] = True
        final_ops = set()
        for k in final_keys:
            w = self.last_writer.get(k)
            if w is not None:
                ops[w]['has_dep'] = True
                final_ops.add(w)
        engs = ['pe', 'act', 'dve', 'pool', 'sp']
        eng_cnt = {e: 0 for e in engs}
        eng_sems = {e: [] for e in engs}
        dma_sems = {}
        dma_cnt = {}
        nsem = [0]

        def new_sem():
            nsem[0] += 1
            return nc.alloc_semaphore("s%d" % nsem[0])

        for o in ops:
            o['ticket'] = None
            if not o['has_dep']:
                continue
            if o['dma_key'] is not None:
                k = o['dma_key']
                if k not in dma_sems:
                    dma_sems[k] = new_sem()
                    dma_cnt[k] = 0
                dma_cnt[k] += 16
                o['ticket'] = (dma_sems[k], dma_cnt[k], ('d', k))
            else:
                e = o['eng']
                c = eng_cnt[e]
                ep = c // EPOCH
                if ep >= len(eng_sems[e]):
                    eng_sems[e].append(new_sem())
                eng_cnt[e] = c + 1
                o['ticket'] = (eng_sems[e][ep], c - ep * EPOCH + 1, (e, ep))
        for o in ops:
            if o['final']:
                sem, val, sid = o['ticket']
                o['ticket'] = (sem, dma_cnt[o['dma_key']], sid)
        self.nsems = nsem[0]
        per_eng = {e: [] for e in engs}
        for i, o in enumerate(ops):
            per_eng[o['eng']].append(i)
        final_waits = [ops[w]['ticket'] for w in sorted(final_ops)]

        def run_engine(ename):
            def body(eng):
                seen = {}
                for i in per_eng[ename]:
                    o = ops[i]
                    need = {}
                    for d in o['deps']:
                        dop = ops[d]
                        if skip(dop, o):
                            continue
                        sem, val, sid = dop['ticket']
                        if seen.get(sid, 0) >= val:
                            continue
                        if sid not in need or need[sid][1] < val:
                            need[sid] = (sem, val)
                    for sid, (sem, val) in need.items():
                        eng.wait_ge(sem, val)
                        seen[sid] = val
                    if getattr(self, "trace", None) is not None:
                        self.trace.append((ename, i, [(sid, v) for sid, (s_, v) in need.items()],
                                           None if o['ticket'] is None else o['ticket'][1:]))
                    ins = o['fn'](eng)
                    if o['ticket'] is not None:
                        ins.then_inc(o['ticket'][0], 16 if o['dma_key'] is not None else 1)
                if ename == 'sp':
                    for (sem, val, sid) in final_waits:
                        if seen.get(sid, 0) < val:
                            eng.wait_ge(sem, val)
                            seen[sid] = val
            return body

        with nc.Block() as block:
            block.tensor(run_engine('pe'))
            block.scalar(run_engine('act'))
            block.vector(run_engine('dve'))
            block.gpsimd(run_engine('pool'))
            block.sync(run_engine('sp'))


class Ring:
    def __init__(self, n):
        self.n = n
        self.i = -1

    def next(self):
        self.i = (self.i + 1) % self.n
        return self.i


def grp(W, wc):
    K, N = W.shape
    return np.ascontiguousarray(W.reshape(K // 128, 128, N // wc, wc).transpose(2, 1, 0, 3))


def featmajor(v):
    sh = v.shape
    n = sh[-1] // 128
    a = v.reshape(sh[:-1] + (n, 128))
    return np.ascontiguousarray(np.moveaxis(a, -1, 0))


def t5_bucket_np(rel):
    nb = 16
    max_exact = 8
    n = -rel
    ret = np.where(n < 0, nb, 0)
    n = np.abs(n)
    nf = np.maximum(n, 1).astype(np.float32)
    large = max_exact + (np.log(nf / max_exact) / math.log(128 / max_exact) * (nb - max_exact)).astype(np.int32)
    large = np.minimum(large, nb - 1)
    return ret + np.where(n < max_exact, n, large)


def weight_shapes():
    return {
        "wqk": [16, 128, 8, 128],
        "wv_m": [4, 128, 8, 256],
        "wo_m": [4, 128, 8, 256],
        "wg_m": [1, 128, 8, 8],
        "wout_m": [8, 128, 8, 128],
        "dq": [8, 128, 8, 128],
        "dk": [8, 128, 8, 128],
        "dv": [2, 128, 8, 512],
        "dwout": [8, 128, 8, 128],
        "wup0": [22, 128, 8, 256],
        "wup1": [22, 128, 8, 256],
        "wdown0": [8, 128, 22, 128],
        "wdown1": [8, 128, 22, 128],
    }


def small_shapes():
    return {
        "cT": [128, 8, 2],
        "b_adaT": [128, 2, 48],
        "lnT": [128, 2, 4, 8],
        "cwm": [128, 16, 5],
        "cwf": [128, 2, 44, 4],
        "normg": [2, 1024],
        "bgate": [1, 8],
        "dlam": [1, 256],
        "biasT": [128, 8, 2, 128],
        "cfar": [1, 8],
        "ident": [128, 128],
        "tri": [128, 128],
    }


def prep_inputs(inputs):
    f = lambda a: np.ascontiguousarray(np.asarray(a, dtype=np.float32))
    x = f(inputs["x"])
    c = f(inputs["c"])
    shared = {}
    m_w_in = f(inputs["m_w_in"])[0]
    shared["wqk"] = grp(m_w_in[:, 0:2048], 128)
    shared["wv_m"] = grp(m_w_in[:, 2048:3072], 256)
    shared["wo_m"] = grp(m_w_in[:, 3072:4096], 256)
    shared["wg_m"] = grp(m_w_in[:, 4096:4104], 8)
    shared["wout_m"] = grp(f(inputs["m_w_out"])[0], 128)
    d_w_in = f(inputs["d_w_in"])[0]
    shared["dq"] = grp(d_w_in[:, 0:1024], 128)
    shared["dk"] = grp(d_w_in[:, 1024:2048], 128)
    shared["dv"] = grp(d_w_in[:, 2048:3072], 512)
    shared["dwout"] = grp(f(inputs["d_w_out"])[0], 128)
    f_w_up = f(inputs["f_w_up"])
    f_conv_w = f(inputs["f_conv_w"])
    f_conv_b = f(inputs["f_conv_b"])
    perm = np.concatenate([np.concatenate([np.arange(128 * g, 128 * g + 128),
                                           DFF + np.arange(128 * g, 128 * g + 128)]) for g in range(NG_FF)])
    cwf = np.zeros((128, 2, 44, 4), np.float32)
    for l in range(2):
        shared["wup%d" % l] = grp(f_w_up[l][:, perm], 256)
        shared["wdown%d" % l] = grp(f(inputs["f_w_down"])[l], 128)
        cw = np.concatenate([f_conv_w[l][:, perm], f_conv_b[l][None, perm]], axis=0)
        cwf[:, l] = featmajor(cw.T.copy().T).transpose(0, 2, 1) if False else \
            np.ascontiguousarray(cw.reshape(4, 44, 128).transpose(2, 1, 0))
    shared["cwf"] = cwf
    mcw = np.concatenate([f(inputs["m_conv_w"])[0], f(inputs["m_conv_b"])[0][None]], axis=0)
    shared["cwm"] = np.ascontiguousarray(mcw.reshape(5, 16, 128).transpose(2, 1, 0))
    shared["w_ada"] = f(inputs["w_ada"])
    shared["b_adaT"] = np.ascontiguousarray(f(inputs["b_ada"]).reshape(2, 48, 128).transpose(2, 0, 1))
    lg = f(inputs["ln_g"]).reshape(4, 8, 128)
    lb = f(inputs["ln_b"]).reshape(4, 8, 128)
    shared["lnT"] = np.ascontiguousarray(np.stack([lg, lb], 0).transpose(3, 0, 1, 2))
    shared["normg"] = np.stack([f(inputs["m_norm_g"])[0], f(inputs["d_norm_g"])[0]], 0)
    shared["bgate"] = f(inputs["m_b_gate"]).reshape(1, 8)
    shared["dlam"] = f(inputs["d_lambda"]).reshape(1, 256)
    rb = f(inputs["rel_bias"])
    kk = np.arange(128)[:, None]
    qq = np.arange(128)[None, :]
    b0 = t5_bucket_np(kk - qq)
    b1 = t5_bucket_np(kk - qq - 128)
    bt = np.stack([rb[b0], rb[b1]], 0)
    shared["biasT"] = np.ascontiguousarray(bt.transpose(1, 3, 0, 2))
    shared["cfar"] = np.ascontiguousarray(rb[15:16, :])
    shared["ident"] = np.eye(128, dtype=np.float32)
    shared["tri"] = np.triu(np.ones((128, 128), np.float32))
    in_maps = []
    for core in range(NCORES):
        m = dict(shared)
        m["x"] = np.ascontiguousarray(x[core * B_PER:(core + 1) * B_PER])
        cc = c[core * B_PER:(core + 1) * B_PER]
        m["cT"] = np.ascontiguousarray(cc.reshape(2, 8, 128).transpose(2, 1, 0))
        in_maps.append(m)
    return in_maps


class K:
    pass


def build_program(subs=((0, 0), (0, 1), (1, 0), (1, 1)), nseq=B_PER, debug_out=None, stop_after=None, skip_conv=False):
    nc = bass.Bass("TRN2", target_bir_lowering=False)
    P = Prog(nc)
    k = K()
    k.nc, k.P = nc, P
    dr = {}
    dr["x"] = nc.dram_tensor("x", [B_PER, S, D], F32, kind="ExternalInput").ap()
    dr["w_ada"] = nc.dram_tensor("w_ada", [2, D, 6 * D], F32, kind="ExternalInput").ap()
    for n, sh in weight_shapes().items():
        dr[n] = nc.dram_tensor(n, sh, F32, kind="ExternalInput").ap()
        dr[n + "_b"] = nc.dram_tensor(n + "_b", sh, BF16, kind="Internal").ap()
    for n, sh in small_shapes().items():
        dr[n] = nc.dram_tensor(n, sh, F32, kind="ExternalInput").ap()
    dr["out"] = nc.dram_tensor("out", [B_PER, S, D], F32, kind="ExternalOutput").ap()
    dr["xres"] = nc.dram_tensor("xres", [B_PER, 128, 8, S], F32, kind="Internal").ap()
    k.dr = dr

    def sb(name, shape, dt=F32):
        return nc.alloc_sbuf_tensor("s_" + name, shape, dt).ap()

    k.sb = sb
    k.ps = [nc.alloc_psum_tensor("ps%d" % i, [128, 512], F32).ap() for i in range(8)]

    def finish_early():
        P.emit(final_keys=[kk for kk in P.last_writer if not isinstance(kk, str) or True])
        return nc, k

    for n in ([] if skip_conv else weight_shapes()):
        src, dst = dr[n], dr[n + "_b"]
        G = src.shape[0]
        for g in range(G):
            P.dma(dst[g], src[g], reads=(), writes=[(n + "_b", g)], key=("cv", n), q='pool', final=True)

    if stop_after == 'conv':
        return finish_early()
    c = K()
    k.c = c
    c.ident = sb("ident", [128, 128])
    c.identb = sb("identb", [128, 128], BF16)
    c.tri = sb("tri", [128, 128])
    c.onesb = sb("onesb", [128, 128], BF16)
    c.ones = sb("ones", [128, 128])
    c.lnT = sb("lnT", [128, 2, 4, 8])
    c.cwm = sb("cwm", [128, 16, 5])
    c.cwf = sb("cwf", [128, 2, 44, 4])
    c.cT = sb("cT", [128, 8, 2])
    c.badaT = sb("badaT", [128, 2, 48])
    P.dma(c.ident, dr["ident"], writes=["ident"], key="c0")
    P.dma(c.tri, dr["tri"], writes=["tri"], key="c1")
    P.dma(c.lnT, dr["lnT"], writes=["lnT"], key="c2")
    P.dma(c.cwm, dr["cwm"], writes=["cwm"], key="c3")
    P.dma(c.cwf, dr["cwf"], writes=["cwf"], key="c4")
    P.dma(c.cT, dr["cT"], writes=["cT"], key="c5")
    P.dma(c.badaT, dr["b_adaT"], writes=["badaT"], key="c6")
    P.copy('dve', c.identb, c.ident, ["ident"], ["identb"])
    P.memset('dve', c.onesb, 1.0 / 1024.0, ["onesb"])
    P.memset('dve', c.ones, 1.0, ["ones"])

    k.UT = sb("UT", [128, 8, S], BF16)
    k.htd = nc.dram_tensor("htd", [128, 8, S], BF16, kind="Internal").ap()
    k.htt = [sb("htt%d" % i, [128, 8, 512], BF16) for i in range(1)]
    k.htt_ring = Ring(1)
    k.xt = [sb("xt%d" % i, [128, 8, 512]) for i in range(1)]
    k.xt_ring = Ring(1)
    k.zb = sb("zb", [128, 8, 512], BF16)
    k.zq = sb("zq", [128, 8, 512], BF16)
    k.msq = sb("msq", [128, 512])
    k.rstd = sb("rstd", [128, 512])
    k.mean = sb("mean", [128, 512])
    k.xo = [sb("xo%d" % i, [128, 8, 512]) for i in range(1)]
    k.xo_ring = Ring(1)
    k.otok = [sb("otok%d" % i, [128, 1024]) for i in range(2)]
    k.otok_ring = Ring(2)
    k.wd = [sb("wd%d" % i, [128, 22, 128], BF16) for i in range(2)]
    k.wd_ring = Ring(2)
    P.fence_ap = sb("fence", [128, 8])
    m = K()
    k.m = m
    m.sh = sb("m_sh", [128, 4, 2, 8])
    m.sc1 = sb("m_sc1", [128, 4, 2, 8])
    m.gt = sb("m_gt", [128, 4, 2, 8])
    m.g1 = sb("m_g1", [128, 4, 2, 8])
    m.b1 = sb("m_b1", [128, 4, 2, 8])
    ada = sb("ada", [128, 2, 48, 2])
    csT = sb("csT", [128, 8, 2])
    k.arena_base = nc.sbuf_base
    print("arena base", k.arena_base, "avail", nc.sbuf_top - nc.sbuf_base)
    k.arena_sets = {}

    P.actv('act', csT, c.cT, AF.Silu, ["cT"], ["csT"])
    k.arena_cur = "startup"
    P.arena_names.update(["wa"])
    wa = [sb("wa%d" % i, [128, 8, 768]) for i in range(2)]
    nc.sbuf_base = k.arena_base
    war = Ring(2)
    wada_v = dr["w_ada"].rearrange("l (kc p) j -> l p kc j", p=128)
    for l in range(2):
        for jg in range(8):
            s_ = war.next()
            P.dma(wa[s_], wada_v[l][:, :, jg * 768:(jg + 1) * 768], writes=[("wa", s_)], key=("wa", s_))
            for jc in range(6):
                j = jg * 6 + jc
                pa = k.ps[j % 2][:, 0:2]
                for kc in range(8):
                    P.mm(pa, wa[s_][:, kc, jc * 128:(jc + 1) * 128], csT[:, kc, :], kc == 0, kc == 7,
                         [("wa", s_), "csT"], [("ps", j % 2)])
                P.ts('dve', ada[:, l, j, :], pa, c.badaT[:, l, j:j + 1], None, ALU.add, None,
                     [("ps", j % 2), "badaT"], ["ada"])
    for l in range(2):
        for s_ in range(2):
            sub = 2 * l + s_
            for b in range(2):
                base = s_ * 24
                P.copy('dve', m.sh[:, sub, b, :], ada[:, l, base:base + 8, b], ["ada"], ["mod"])
                P.ts('dve', m.sc1[:, sub, b, :], ada[:, l, base + 8:base + 16, b], 1.0, None, ALU.add, None,
                     ["ada"], ["mod"])
                P.ts('dve', m.gt[:, sub, b, :], ada[:, l, base + 16:base + 24, b], 1.0, 1.0 / ALPHA,
                     ALU.add, ALU.mult, ["ada"], ["mod"])
    for sub in range(3):
        for b in range(2):
            P.tt('dve', m.g1[:, sub, b, :], c.lnT[:, 0, sub, :], m.sc1[:, sub + 1, b, :], ALU.mult,
                 ["lnT", "mod"], ["mod2"])
            P.tt('dve', m.b1[:, sub, b, :], c.lnT[:, 1, sub, :], m.sc1[:, sub + 1, b, :], ALU.mult,
                 ["lnT", "mod"], ["mod2"])
            P.tt('dve', m.b1[:, sub, b, :], m.b1[:, sub, b, :], m.sh[:, sub + 1, b, :], ALU.add,
                 ["mod", "mod2"], ["mod2"])

    if stop_after == 'ada':
        return finish_early()
    first_sub = 2 * subs[0][0] + subs[0][1]
    last_sub = 2 * subs[-1][0] + subs[-1][1]
    for b in range(nseq):
        load_input(k, b, first_sub)
        if stop_after == 'load':
            return finish_early()
        for (l, s_) in subs:
            sub = 2 * l + s_
            if s_ == 1:
                ffn_sublayer(k, l, b, sub, is_last=(sub == last_sub))
            elif l == 0:
                mlstm_sublayer(k, b, sub, is_last=(sub == last_sub))
            else:
                attn_sublayer(k, b, sub, is_last=(sub == last_sub))
    P.emit(final_keys=[("out", b, t) for b in range(nseq) for t in range(16)])
    return nc, k


def arena_set(k, name, alloc_fn):
    if k.arena_cur != name:
        k.P.fence_arena()
        k.arena_cur = name
    if name not in k.arena_sets:
        k.nc.sbuf_base = k.arena_base
        k.arena_sets[name] = alloc_fn()
    return k.arena_sets[name]


def load_input(k, b, sub):
    import os
    P, c, m = k.P, k.c, k.m
    dbg = os.environ.get("DBG", "")
    for tb in range(int(os.environ.get("NTB", "16"))):
        s_ = k.otok_ring.next()
        xin = k.otok[s_]
        P.dma(xin, k.dr["x"][b, tb * 128:(tb + 1) * 128, :], reads=(), writes=[("otok", s_)], key=("otok", s_))
        xs = k.xo_ring.next()
        xo = k.xo[xs]
        for half in range(2):
            bank = 6 + half
            for j in range(4):
                dc = half * 4 + j
                P.tr(k.ps[bank][:, j * 128:(j + 1) * 128], xin[:, dc * 128:(dc + 1) * 128], c.ident,
                     [("otok", s_), "ident"], [("ps", bank)])
            if "noact" not in dbg:
                P.copy('act', xo[:, half * 4:half * 4 + 4, 0:128],
                       k.ps[bank].rearrange("p (a b) -> p a b", a=4), [("ps", bank)], [("xo", xs)])
            for j in range(4):
                if "nodve" in dbg:
                    break
                dc = half * 4 + j
                P.ts('dve', k.UT[:, dc, tb * 128:(tb + 1) * 128], k.ps[bank][:, j * 128:(j + 1) * 128],
                     m.sc1[:, sub, b, dc:dc + 1], m.sh[:, sub, b, dc:dc + 1], ALU.mult, ALU.add,
                     [("ps", bank), "mod"], [("UT", tb // 4)])
        if "nost" not in dbg:
            P.dma(k.dr["xres"][b][:, :, tb * 128:(tb + 1) * 128], xo[:, :, 0:128], reads=[("xo", xs)],
                  writes=[("xres", b, tb // 4)], key=("xo", xs))


def ln_epilogue(k, b, sub, tt, ybank_fn, is_last):
    P, c, m = k.P, k.c, k.m
    tok = slice(tt * 512, (tt + 1) * 512)
    zs = k.xt_ring.next()
    z = k.xt[zs]
    P.dma(z, k.dr["xres"][b][:, :, tok], reads=[("xres", b, tt)], writes=[("xt", zs)], key=("xt", zs))
    for dc in range(8):
        bank = ybank_fn(dc)
        P.stt('dve', z[:, dc, :], k.ps[bank], m.gt[:, sub, b, dc:dc + 1], z[:, dc, :], ALU.mult, ALU.add,
              [("ps", bank), "mod", ("xt", zs)], [("xt", zs)])
    P.copy('act', k.zb, z, [("xt", zs)], ["zb"])
    P.actv('act', k.zq, z, AF.Square, [("xt", zs)], ["zq"])
    for dc in range(8):
        P.mm(k.ps[6], c.onesb, k.zb[:, dc, :], dc == 0, dc == 7, ["onesb", "zb"], [("ps", 6)])
    for dc in range(8):
        P.mm(k.ps[7], c.onesb, k.zq[:, dc, :], dc == 0, dc == 7, ["onesb", "zq"], [("ps", 7)])
    P.copy('act', k.mean, k.ps[6], [("ps", 6)], ["mean"])
    P.tt('dve', k.msq, k.mean, k.mean, ALU.mult, ["mean"], ["msq"])
    P.tt('dve', k.rstd, k.ps[7], k.msq, ALU.subtract, [("ps", 7), "msq"], ["rstd"])
    P.ts('dve', k.rstd, k.rstd, EPS_LN, None, ALU.add, None, ["rstd"], ["rstd"])
    P.actv('act', k.rstd, k.rstd, AF.Sqrt, ["rstd"], ["rstd"])
    P.op('dve', lambda e: e.reciprocal(out=k.rstd, in_=k.rstd), ["rstd"], ["rstd"])
    P.tt('dve', z, z, k.mean.unsqueeze(1).to_broadcast([128, 8, 512]), ALU.subtract,
         [("xt", zs), "mean"], [("xt", zs)])
    P.tt('pool', z, z, k.rstd.unsqueeze(1).to_broadcast([128, 8, 512]), ALU.mult,
         [("xt", zs), "rstd"], [("xt", zs)])
    xs = k.xo_ring.next()
    xo = k.xo[xs]
    for dc in range(8):
        P.actv('act', xo[:, dc, :], z[:, dc, :], AF.Identity, [("xt", zs), "lnT"], [("xo", xs)],
               bias=c.lnT[:, 1, sub, dc:dc + 1], scale=c.lnT[:, 0, sub, dc:dc + 1])
    if not is_last:
        for dc in range(8):
            P.ts('dve', k.UT[:, dc, tok], z[:, dc, :], m.g1[:, sub, b, dc:dc + 1], m.b1[:, sub, b, dc:dc + 1],
                 ALU.mult, ALU.add, [("xt", zs), "mod2"], [("UT", tt)])
        P.dma(k.dr["xres"][b][:, :, tok], xo, reads=[("xo", xs)], writes=[("xres", b, tt)], key=("xo", xs))
    else:
        for j4 in range(4):
            tb = tt * 4 + j4
            os_ = k.otok_ring.next()
            ot = k.otok[os_]
            for half in range(2):
                bank = 6 + half
                for j in range(4):
                    dc = half * 4 + j
                    P.tr(k.ps[bank][:, j * 128:(j + 1) * 128], xo[:, dc, j4 * 128:(j4 + 1) * 128], c.ident,
                         [("xo", xs), "ident"], [("ps", bank)])
                P.copy('act' if half == 0 else 'dve', ot[:, half * 512:(half + 1) * 512], k.ps[bank],
                       [("ps", bank)], [("otok", os_)])
            P.dma(k.dr["out"][b, tb * 128:(tb + 1) * 128, :], ot, reads=[("otok", os_)],
                  writes=[("out", b, tb)], key=("otok", os_))


def outproj_ln(k, b, sub, wname, is_last):
    P = k.P
    W = k.dr[wname + "_b"]
    yr = Ring(2)
    for tt in range(4):
        tok = slice(tt * 512, (tt + 1) * 512)
        hs_ = k.htt_ring.next()
        ht = k.htt[hs_]
        P.dma(ht, k.htd[:, :, tok], reads=[("htd", tt)], writes=[("htt", hs_)], key=("htt", hs_))

        def ybank(dc, tt=tt, ht=ht, hs_=hs_):
            ws = k.wd_ring.next()
            wt = k.wd[ws]
            P.dma(wt[:, 0:8, :], W[dc], reads=[(wname + "_b", dc)], writes=[("wd", ws)], key=("wd", ws))
            bank = 4 + yr.next()
            for kc in range(8):
                P.mm(k.ps[bank], wt[:, kc, :], ht[:, kc, :], kc == 0, kc == 7,
                     [("wd", ws), ("htt", hs_)], [("ps", bank)])
            return bank

        ln_epilogue(k, b, sub, tt, ybank, is_last)


def ffn_sublayer(k, l, b, sub, is_last):
    P, c = k.P, k.c
    def alloc():
        f = K()
        k.P.arena_names.update(["wup", "G", "hs", "acc", "halo"])
        f.wup = [k.sb("wup%d" % i, [128, 8, 256], BF16) for i in range(3)]
        f.wup_ring = Ring(3)
        f.G = k.sb("G", [128, NG_FF, 512], BF16)
        f.hs = [k.sb("hs%d" % i, [128, 514]) for i in range(4)]
        f.hs_ring = Ring(4)
        f.acc = [k.sb("acc%d" % i, [128, 512]) for i in range(4)]
        f.acc_ring = Ring(4)
        f.halo = k.sb("halo", [128, 44, 2])
        f.up_ring = Ring(4)
        return f

    f = arena_set(k, "ffn", alloc)
    WU = k.dr["wup%d_b" % l]
    WD = k.dr["wdown%d_b" % l]
    for tt in range(4):
        tok = slice(tt * 512, (tt + 1) * 512)
        for g in range(NG_FF):
            ws = f.wup_ring.next()
            wt = f.wup[ws]
            P.dma(wt, WU[g], reads=[("wup%d_b" % l, g)], writes=[("wup", ws)], key=("wup", ws))
            accs = []
            for half in range(2):
                idx = 2 * g + half
                bank = f.up_ring.next()
                pb = k.ps[bank]
                for kc in range(8):
                    P.mm(pb, wt[:, kc, half * 128:(half + 1) * 128], k.UT[:, kc, tok], kc == 0, kc == 7,
                         [("wup", ws), ("UT", tt)], [("ps", bank)])
                hs_i = f.hs_ring.next()
                hs = f.hs[hs_i]
                ac_i = f.acc_ring.next()
                acc = f.acc[ac_i]
                cw = c.cwf[:, l, idx, :]
                P.copy('act', hs[:, 2:514], pb, [("ps", bank)], [("hs", hs_i)])
                if tt == 0:
                    P.memset('pool', hs[:, 0:2], 0.0, [("hs", hs_i)])
                else:
                    P.copy('pool', hs[:, 0:2], f.halo[:, idx, :], [("halo", idx)], [("hs", hs_i)])
                if half == 0:
                    P.ts('dve', acc, hs[:, 2:514], cw[:, 2:3], cw[:, 3:4], ALU.mult, ALU.add,
                         [("hs", hs_i), "cwf"], [("acc", ac_i)])
                else:
                    P.actv('act', acc, pb, AF.Identity, [("ps", bank), "cwf"], [("acc", ac_i)],
                           bias=cw[:, 3:4], scale=cw[:, 2:3])
                P.stt('dve', acc, hs[:, 1:513], cw[:, 1:2], acc, ALU.mult, ALU.add,
                      [("hs", hs_i), "cwf", ("acc", ac_i)], [("acc", ac_i)])
                P.stt('dve', acc, hs[:, 0:512], cw[:, 0:1], acc, ALU.mult, ALU.add,
                      [("hs", hs_i), "cwf", ("acc", ac_i)], [("acc", ac_i)])
                if tt < 3:
                    P.copy('pool', f.halo[:, idx, :], hs[:, 512:514], [("hs", hs_i)], [("halo", idx)])
                if half == 1:
                    P.actv('act', acc, acc, AF.Gelu, [("acc", ac_i)], [("acc", ac_i)])
                accs.append((acc, ac_i))
            P.tt('pool', f.G[:, g, :], accs[0][0], accs[1][0], ALU.mult,
                 [("acc", accs[0][1]), ("acc", accs[1][1])], [("G", g)])
        yr = Ring(2)

        def ybank(dc, tt=tt):
            ws = k.wd_ring.next()
            wt = k.wd[ws]
            P.dma(wt, WD[dc], reads=[("wdown%d_b" % l, dc)], writes=[("wd", ws)], key=("wd", ws))
            bank = 4 + yr.next()
            for kc in range(NG_FF):
                P.mm(k.ps[bank], wt[:, kc, :], f.G[:, kc, :], kc == 0, kc == NG_FF - 1,
                     [("wd", ws), ("G", kc)], [("ps", bank)])
            return bank

        ln_epilogue(k, b, sub, tt, ybank, is_last)


def mlstm_sublayer(k, b, sub, is_last):
    P, c, nc = k.P, k.c, k.nc
    dr = k.dr

    def alloc():
        a = K()
        P.arena_names.update(["m_w", "m_wv", "m_wo", "m_wg", "m_hs", "m_acc", "m_qT", "m_kT", "m_v", "m_so",
                              "m_kt", "m_C", "m_Cb", "m_g", "m_aT", "m_kw", "m_h", "m_hb", "m_hst", "m_ng",
                              "m_sm", "m_junk"])
        a.w = [k.sb("m_w%d" % i, [128, 8, 128], BF16) for i in range(2)]
        a.wv = k.sb("m_wv", [128, 8, 256], BF16)
        a.wo = k.sb("m_wo", [128, 8, 256], BF16)
        a.wg = k.sb("m_wg", [128, 8, 8], BF16)
        a.hs = k.sb("m_hs", [128, 3 + S])
        a.acc = k.sb("m_acc", [128, S])
        a.qT = k.sb("m_qT", [128, 2, S], BF16)
        a.kT = k.sb("m_kT", [128, 2, S], BF16)
        a.v = k.sb("m_v", [128, 16, 264], BF16)
        a.so = k.sb("m_so", [128, 16, 256], BF16)
        a.kt = k.sb("m_kt", [128, 16, 256], BF16)
        a.C = k.sb("m_C", [128, 2, 264])
        a.Cb = k.sb("m_Cb", [128, 2, 264], BF16)
        a.G = k.sb("m_G", [128, 16, 8])
        a.bg = k.sb("m_bg", [128, 8])
        a.sp = k.sb("m_sp", [128, 64])
        a.csp = k.sb("m_csp", [128, 64])
        a.bsp = k.sb("m_bsp", [128, 64])
        a.as_ = k.sb("m_as", [128, 64])
        a.amr = k.sb("m_amr", [128, 64])
        a.amc = k.sb("m_amc", [128, 2])
        a.dg = k.sb("m_dg", [128, 64])
        a.M = k.sb("m_M", [128, 17, 4])
        a.tmp = k.sb("m_tmp", [128, 64])
        a.es = k.sb("m_es", [128, 64])
        a.ws = k.sb("m_ws", [128, 64])
        a.dec = k.sb("m_dec", [128, 64])
        a.flo = k.sb("m_flo", [128, 64])
        a.aT = [k.sb("m_aT%d" % i, [128, 128], BF16) for i in range(2)]
        a.kw = [k.sb("m_kw%d" % i, [128, 256], BF16) for i in range(2)]
        a.h = [k.sb("m_h%d" % i, [128, 256]) for i in range(2)]
        a.hb = [k.sb("m_hb%d" % i, [128, 256], BF16) for i in range(2)]
        a.junk = k.sb("m_junk", [128, 256], BF16)
        a.hst = [k.sb("m_hst%d" % i, [128, 2, 512], BF16) for i in range(2)]
        a.ng = k.sb("m_ng", [128, 1024])
        a.sm = k.sb("m_sm", [128, 16])
        a.rings = dict(w=Ring(2), aT=Ring(2), kw=Ring(2), h=Ring(2), hst=Ring(2))
        return a

    a = arena_set(k, "mlstm", alloc)
    G_KEYS = ["m_g"]
    P.dma(a.ng, dr["normg"][0:1, :].partition_broadcast(128), writes=["m_ng"], key="m_c0")
    P.dma(a.bg, dr["bgate"].partition_broadcast(128), writes=["m_g"], key="m_c1")
    P.dma(a.wg, dr["wg_m_b"][0], reads=[("wg_m_b", 0)], writes=["m_wg"], key="m_c2")
    P.memset('pool', a.hs[:, 0:3], 0.0, ["m_hs"])
    P.memset('pool', a.v[:, :, 256:257], 1.0, ["m_v"])

    for cch in range(16):
        for kc in range(8):
            P.mm(k.ps[0][:, cch * 8:(cch + 1) * 8], k.UT[:, kc, cch * 128:(cch + 1) * 128], a.wg[:, kc, :],
                 kc == 0, kc == 7, [("UT", cch // 4), "m_wg"], [("ps", 0)])
    P.tt('dve', a.G, k.ps[0][:, 0:128].rearrange("p (a b) -> p a b", a=16),
         a.bg.unsqueeze(1).to_broadcast([128, 16, 8]), ALU.add, [("ps", 0), "m_g"], G_KEYS)
    sp3 = a.sp.rearrange("p (a b) -> p a b", a=16)
    as3 = a.as_.rearrange("p (a b) -> p a b", a=16)
    P.actv('act', sp3, a.G[:, :, 4:8], AF.Exp, G_KEYS, G_KEYS, scale=-1.0)
    P.actv('act', a.sp, a.sp, AF.Ln, G_KEYS, G_KEYS, bias=1.0)
    P.mm(k.ps[1][:, 0:64], c.tri, a.sp, True, True, ["tri"] + G_KEYS, [("ps", 1)])
    P.mm(k.ps[1][:, 64:128], c.ones, a.sp, True, True, ["ones"] + G_KEYS, [("ps", 1)])
    P.copy('dve', a.csp, k.ps[1][:, 0:64], [("ps", 1)], G_KEYS)
    P.copy('dve', a.bsp, k.ps[1][:, 64:128], [("ps", 1)], G_KEYS)
    P.tt('dve', as3, a.G[:, :, 0:4], a.csp.rearrange("p (a b) -> p a b", a=16), ALU.add, G_KEYS, G_KEYS)
    P.tr(k.ps[2][0:64, 0:128], a.as_, c.ident, G_KEYS + ["ident"], [("ps", 2)])
    P.op('dve', lambda e: e.reduce_max(out=a.amc[0:64, 0:1], in_=k.ps[2][0:64, 0:128], axis=AX.X),
         [("ps", 2)], G_KEYS)
    P.ts('dve', a.dg[0:64, :], c.ident[0:64, 0:64], a.amc[0:64, 0:1], None, ALU.mult, None, ["ident"] + G_KEYS, G_KEYS)
    P.mm(k.ps[2][:, 128:192], c.ones[0:64, :], a.dg[0:64, :], True, True, ["ones"] + G_KEYS, [("ps", 2)])
    P.copy('dve', a.amr, k.ps[2][:, 128:192], [("ps", 2)], G_KEYS)
    amr3 = a.amr.rearrange("p (a b) -> p a b", a=16)
    bsp3 = a.bsp.rearrange("p (a b) -> p a b", a=16)
    P.memset('dve', a.M[:, 0, :], 0.0, G_KEYS)
    for cch in range(16):
        P.tt('dve', a.tmp[:, 0:4], a.M[:, cch, :], amr3[:, cch, :], ALU.max, G_KEYS, G_KEYS)
        P.tt('dve', a.M[:, cch + 1, :], a.tmp[:, 0:4], bsp3[:, cch, :], ALU.subtract, G_KEYS, G_KEYS)
    mprev = a.M[:, 0:16, :]
    mnew = a.M[:, 1:17, :]
    tmp3 = a.tmp.rearrange("p (a b) -> p a b", a=16)
    es3 = a.es.rearrange("p (a b) -> p a b", a=16)
    ws3 = a.ws.rearrange("p (a b) -> p a b", a=16)
    dec3 = a.dec.rearrange("p (a b) -> p a b", a=16)
    flo3 = a.flo.rearrange("p (a b) -> p a b", a=16)
    csp3 = a.csp.rearrange("p (a b) -> p a b", a=16)
    P.tt('dve', tmp3, as3, mprev, ALU.subtract, G_KEYS, G_KEYS)
    P.actv('act', es3, tmp3, AF.Exp, G_KEYS, G_KEYS)
    P.tt('dve', tmp3, as3, bsp3, ALU.subtract, G_KEYS, G_KEYS)
    P.tt('dve', tmp3, tmp3, mnew, ALU.subtract, G_KEYS, G_KEYS)
    P.actv('act', ws3, tmp3, AF.Exp, G_KEYS, G_KEYS)
    P.tt('dve', tmp3, mprev, bsp3, ALU.subtract, G_KEYS, G_KEYS)
    P.tt('dve', tmp3, tmp3, mnew, ALU.subtract, G_KEYS, G_KEYS)
    P.actv('act', dec3, tmp3, AF.Exp, G_KEYS, G_KEYS)
    P.tt('dve', tmp3, csp3, mprev, ALU.subtract, G_KEYS, G_KEYS)
    P.actv('act', flo3, tmp3, AF.Exp, G_KEYS, G_KEYS, bias=math.log(16.0))

    bf7 = k.ps[7].bitcast(BF16)
    pr = Ring(2)
    for h in range(4):
        for which, dst in ((0, a.qT), (1, a.kT)):
            for dc in range(2):
                chunk = which * 8 + 2 * h + dc
                ws_ = a.rings['w'].next()
                P.dma(a.w[ws_], dr["wqk_b"][chunk], reads=[("wqk_b", chunk)], writes=[("m_w", ws_)], key=("m_w", ws_))
                for tt in range(4):
                    bank = 4 + pr.next()
                    for kc in range(8):
                        P.mm(k.ps[bank], a.w[ws_][:, kc, :], k.UT[:, kc, tt * 512:(tt + 1) * 512], kc == 0, kc == 7,
                             [("m_w", ws_), ("UT", tt)], [("ps", bank)])
                    P.copy('act', a.hs[:, 3 + tt * 512:3 + (tt + 1) * 512], k.ps[bank], [("ps", bank)], ["m_hs"])
                cw = c.cwm[:, chunk, :]
                P.ts('dve', a.acc, a.hs[:, 3:3 + S], cw[:, 3:4], cw[:, 4:5], ALU.mult, ALU.add, ["m_hs", "cwm"], ["m_acc"])
                P.stt('dve', a.acc, a.hs[:, 2:2 + S], cw[:, 2:3], a.acc, ALU.mult, ALU.add, ["m_hs", "cwm", "m_acc"], ["m_acc"])
                P.stt('dve', a.acc, a.hs[:, 1:1 + S], cw[:, 1:2], a.acc, ALU.mult, ALU.add, ["m_hs", "cwm", "m_acc"], ["m_acc"])
                P.stt('dve', a.acc, a.hs[:, 0:S], cw[:, 0:1], a.acc, ALU.mult, ALU.add, ["m_hs", "cwm", "m_acc"], ["m_acc"])
                P.actv('act', dst[:, dc, :], a.acc, AF.Silu, ["m_acc"], ["m_qT" if which == 0 else "m_kT"])
        P.dma(a.wv, dr["wv_m_b"][h], reads=[("wv_m_b", h)], writes=["m_wv"], key="m_wv")
        P.dma(a.wo, dr["wo_m_b"][h], reads=[("wo_m_b", h)], writes=["m_wo"], key="m_wo")
        for wt, wkey, dst, dkey, fn in ((a.wv, "m_wv", a.v, "m_v", None), (a.wo, "m_wo", a.so, "m_so", AF.Sigmoid)):
            for c2 in range(8):
                bank = 4 + pr.next()
                for j in range(2):
                    cch = 2 * c2 + j
                    for kc in range(8):
                        P.mm(k.ps[bank][:, j * 256:(j + 1) * 256], k.UT[:, kc, cch * 128:(cch + 1) * 128], wt[:, kc, :],
                             kc == 0, kc == 7, [("UT", cch // 4), wkey], [("ps", bank)])
                src = k.ps[bank].rearrange("p (a b) -> p a b", a=2)
                if fn is None:
                    P.copy('dve', dst[:, 2 * c2:2 * c2 + 2, 0:256], src, [("ps", bank)], [dkey])
                else:
                    P.actv('act', dst[:, 2 * c2:2 * c2 + 2, :], src, fn, [("ps", bank)], [dkey])
        for c2 in range(8):
            for j in range(2):
                cch = 2 * c2 + j
                for dc in range(2):
                    P.tr(bf7[:, j * 256 + dc * 128:j * 256 + (dc + 1) * 128], a.kT[:, dc, cch * 128:(cch + 1) * 128],
                         c.identb, ["m_kT", "identb"], [("ps", 7)])
            P.copy('dve', a.kt[:, 2 * c2:2 * c2 + 2, :], bf7[:, 0:512].rearrange("p (a b) -> p a b", a=2),
                   [("ps", 7)], ["m_kt"])
        P.memset('pool', a.C, 0.0, ["m_C"])
        P.memset('pool', a.Cb, 0.0, ["m_Cb"])
        for cch in range(16):
            col = cch * 4 + h
            ctile = slice(cch * 128, (cch + 1) * 128)
            for dc in range(2):
                P.mm(k.ps[0][:, 0:128], a.kT[:, dc, ctile], a.qT[:, dc, ctile], dc == 0, dc == 1,
                     ["m_kT", "m_qT"], [("ps", 0)])
            ai = a.rings['aT'].next()
            P.stt('dve', a.aT[ai], k.ps[0][:, 0:128], a.es[:, col:col + 1], c.tri, ALU.mult, ALU.mult,
                  [("ps", 0), "tri"] + G_KEYS, [("m_aT", ai)])
            num = k.ps[1][:, 0:257]
            P.mm(num, a.aT[ai], a.v[:, cch, 0:257], True, cch == 0, [("m_aT", ai), "m_v"], [("ps", 1)])
            if cch > 0:
                for dc in range(2):
                    P.mm(num, a.qT[:, dc, ctile], a.Cb[:, dc, 0:257], False, dc == 1, ["m_qT", "m_Cb"], [("ps", 1)])
            sm = a.sm
            P.actv('act', sm[:, 0:1], num[:, 256:257], AF.Abs, [("ps", 1)], ["m_sm"])
            P.tt('dve', sm[:, 0:1], sm[:, 0:1], a.flo[:, col:col + 1], ALU.max, ["m_sm"] + G_KEYS, ["m_sm"])
            P.op('dve', lambda e, o_=sm[:, 0:1]: e.reciprocal(out=o_, in_=o_), ["m_sm"], ["m_sm"])
            hi = a.rings['h'].next()
            hh = a.h[hi]
            P.ts('dve', hh, num[:, 0:256], sm[:, 0:1], None, ALU.mult, None, [("ps", 1), "m_sm"], [("m_h", hi)])
            P.actv('act', a.junk, hh, AF.Square, [("m_h", hi)], ["m_junk", "m_sm2"], accum=sm[:, 1:2])
            P.ts('dve', sm[:, 1:2], sm[:, 1:2], 1.0 / 256.0, LN_EPS, ALU.mult, ALU.add, ["m_sm2"], ["m_sm2"])
            P.actv('act', sm[:, 1:2], sm[:, 1:2], AF.Sqrt, ["m_sm2"], ["m_sm2"])
            P.op('dve', lambda e, o_=sm[:, 1:2]: e.reciprocal(out=o_, in_=o_), ["m_sm2"], ["m_sm2"])
            P.stt('dve', hh, hh, sm[:, 1:2], a.ng[:, h * 256:(h + 1) * 256], ALU.mult, ALU.mult,
                  [("m_h", hi), "m_sm2", "m_ng"], [("m_h", hi)])
            P.tt('pool', a.hb[hi], hh, a.so[:, cch, :], ALU.mult, [("m_h", hi), "m_so"], [("m_hb", hi)])
            if cch % 4 == 0:
                hst_i = a.rings['hst'].next()
            hst = a.hst[hst_i]
            for dc in range(2):
                P.tr(bf7[:, 512 + dc * 128:512 + (dc + 1) * 128], a.hb[hi][:, dc * 128:(dc + 1) * 128], c.identb,
                     [("m_hb", hi), "identb"], [("ps", 7)])
            P.copy('act', hst[:, :, (cch % 4) * 128:(cch % 4 + 1) * 128],
                   bf7[:, 512:768].rearrange("p (a b) -> p a b", a=2), [("ps", 7)], [("m_hst", hst_i)])
            if cch % 4 == 3:
                g4 = cch // 4
                P.dma(k.htd[:, 2 * h:2 * h + 2, g4 * 512:(g4 + 1) * 512], hst, reads=[("m_hst", hst_i)],
                      writes=[("htd", g4)], key=("m_hst", hst_i))
            if cch < 15:
                ki = a.rings['kw'].next()
                P.ts('pool', a.kw[ki], a.kt[:, cch, :], a.ws[:, col:col + 1], None, ALU.mult, None,
                     ["m_kt"] + G_KEYS, [("m_kw", ki)])
                for dc in range(2):
                    P.mm(k.ps[2 + dc][:, 0:257], a.kw[ki][:, dc * 128:(dc + 1) * 128], a.v[:, cch, 0:257], True, True,
                         [("m_kw", ki), "m_v"], [("ps", 2 + dc)])
                    P.stt('dve', a.C[:, dc, 0:257], a.C[:, dc, 0:257], a.dec[:, col:col + 1], k.ps[2 + dc][:, 0:257],
                          ALU.mult, ALU.add, ["m_C", ("ps", 2 + dc)] + G_KEYS, ["m_C"])
                P.copy('act', a.Cb, a.C, ["m_C"], ["m_Cb"])
    outproj_ln(k, b, sub, "wout_m", is_last)


LAM_INIT1 = 0.8 - 0.6 * math.exp(-0.3 * 1)


def attn_sublayer(k, b, sub, is_last):
    P, c, nc = k.P, k.c, k.nc
    dr = k.dr

    def alloc():
        a = K()
        P.arena_names.update(["a_wq", "a_wk", "a_wv", "a_qT", "a_kT", "a_v", "a_sq", "a_pT", "a_t0", "a_o",
                              "a_hst", "a_small", "a_eb", "a_ng", "a_sel", "a_mx"])
        a.wq = [k.sb("a_wq%d" % i, [128, 8, 128], BF16) for i in range(1)]
        a.wk = [k.sb("a_wk%d" % i, [128, 8, 128], BF16) for i in range(1)]
        a.wv = k.sb("a_wv", [128, 8, 512], BF16)
        a.qT = [k.sb("a_qT%d" % i, [128, S], BF16) for i in range(1)]
        a.kT = [k.sb("a_kT%d" % i, [128, S], BF16) for i in range(1)]
        a.v = k.sb("a_v", [128, 16, 8, 130], BF16)
        a.sq = [k.sb("a_sq%d" % i, [128, 512], BF16) for i in range(2)]
        a.pT = [k.sb("a_pT%d" % i, [128, 512], BF16) for i in range(4)]
        a.t0 = k.sb("a_t0", [128, 4, 128])
        a.o = [k.sb("a_o%d" % i, [128, 128]) for i in range(2)]
        a.ob = [k.sb("a_ob%d" % i, [128, 128], BF16) for i in range(2)]
        a.hst = [k.sb("a_hst%d" % i, [128, 512], BF16) for i in range(2)]
        a.small = k.sb("a_small", [128, 64])
        a.eb = k.sb("a_eb", [128, 8, 2, 128])
        a.ng = k.sb("a_ng", [128, 1024])
        a.lam = k.sb("a_lam", [128, 256])
        a.cfar = k.sb("a_cfar", [128, 8])
        a.sel = [k.sb("a_sel%d" % i, [128, 128], BF16) for i in range(2)]
        a.mx = k.sb("a_mx", [128, 2, 2, 4])
        a.rings = dict(w=Ring(1), qk=Ring(1), sq=Ring(2), pT=Ring(4), o=Ring(2), hst=Ring(2), s=Ring(2))
        a.init = False
        return a

    a = arena_set(k, "attn", alloc)
    sm = a.small
    C_E1, C_E2, C_NEGLAM, C_BIAS, C_R, C_SS, C_MQ, C_MK, C_M = 0, 1, 2, 4, 8, 10, 12, 14, 16

    P.dma(a.lam, dr["dlam"].partition_broadcast(128), writes=["a_small"], key="a_c0")
    P.dma(a.cfar, dr["cfar"].partition_broadcast(128), writes=["a_small"], key="a_c1")
    P.dma(a.ng, dr["normg"][1:2, :].partition_broadcast(128), writes=["a_ng"], key="a_c2")
    P.dma(a.eb, dr["biasT"], writes=["a_eb"], key="a_c3")
    P.ts('dve', a.ng, a.ng, 1.0 - LAM_INIT1, None, ALU.mult, None, ["a_ng"], ["a_ng"])
    P.tt('dve', a.lam[:, 0:64], a.lam[:, 0:64], a.lam[:, 64:128], ALU.mult, ["a_small"], ["a_small"])
    P.tt('dve', a.lam[:, 128:192], a.lam[:, 128:192], a.lam[:, 192:256], ALU.mult, ["a_small"], ["a_small"])
    P.op('dve', lambda e: e.reduce_sum(out=sm[:, C_E1:C_E1 + 1], in_=a.lam[:, 0:64], axis=AX.X), ["a_small"], ["a_small"])
    P.op('dve', lambda e: e.reduce_sum(out=sm[:, C_E2:C_E2 + 1], in_=a.lam[:, 128:192], axis=AX.X), ["a_small"], ["a_small"])
    P.actv('act', sm[:, C_E1:C_E1 + 2], sm[:, C_E1:C_E1 + 2], AF.Exp, ["a_small"], ["a_small"])
    P.tt('dve', sm[:, C_NEGLAM:C_NEGLAM + 1], sm[:, C_E2:C_E2 + 1], sm[:, C_E1:C_E1 + 1], ALU.subtract,
         ["a_small"], ["a_small"])
    P.ts('dve', sm[:, C_NEGLAM:C_NEGLAM + 1], sm[:, C_NEGLAM:C_NEGLAM + 1], -LAM_INIT1, None, ALU.add, None,
         ["a_small"], ["a_small"])
    P.ts('dve', a.cfar, a.cfar, -1.0, None, ALU.mult, None, ["a_small"], ["a_small"])
    for h in range(8):
        P.actv('act', a.eb[:, h], a.eb[:, h], AF.Exp, ["a_eb", "a_small"], ["a_eb"], bias=a.cfar[:, h:h + 1])
    P.memset('pool', a.eb[64:128, :, 0, 0:64], 0.0, ["a_eb"])
    P.ts('dve', a.cfar, a.cfar, -1.0, None, ALU.mult, None, ["a_small"], ["a_small"])
    P.memset('pool', a.sel[0], 0.0, ["a_sel"])
    P.memset('pool', a.sel[1], 0.0, ["a_sel"])
    P.memset('pool', a.sel[0][0:64, :], 1.0, ["a_sel"])
    P.memset('pool', a.sel[1][64:128, :], 1.0, ["a_sel"])
    P.memset('pool', a.v[:, :, :, 128:129], 1.0, [("a_v", i) for i in range(16)])

    for grp_ in range(2):
        P.dma(a.wv, dr["dv_b"][grp_], reads=[("dv_b", grp_)], writes=["a_wv"], key="a_wv")
        for tb in range(16):
            bank = a.rings['s'].next()
            for kc in range(8):
                P.mm(k.ps[bank], k.UT[:, kc, tb * 128:(tb + 1) * 128], a.wv[:, kc, :], kc == 0, kc == 7,
                     [("UT", tb // 4), "a_wv"], [("ps", bank)])
            P.copy('act' if tb % 2 == 0 else 'dve', a.v[:, tb, grp_ * 4:grp_ * 4 + 4, 0:128],
                   k.ps[bank].rearrange("p (a b) -> p a b", a=4), [("ps", bank)], [("a_v", tb)])

    bf7 = k.ps[7].bitcast(BF16)
    for h in range(8):
        ws = a.rings['w'].next()
        P.dma(a.wq[ws], dr["dq_b"][h], reads=[("dq_b", h)], writes=[("a_wq", ws)], key=("a_wq", ws))
        P.dma(a.wk[ws], dr["dk_b"][h], reads=[("dk_b", h)], writes=[("a_wk", ws)], key=("a_wk", ws))
        qs = a.rings['qk'].next()
        qT, kT = a.qT[qs], a.kT[qs]
        for which, (wt, dst, wkey, dkey) in enumerate(((a.wq[ws], qT, "a_wq", "a_qT"), (a.wk[ws], kT, "a_wk", "a_kT"))):
            for tt in range(4):
                tok = slice(tt * 512, (tt + 1) * 512)
                bank = a.rings['s'].next()
                for kc in range(8):
                    P.mm(k.ps[bank], wt[:, kc, :], k.UT[:, kc, tok], kc == 0, kc == 7,
                         [(wkey, ws), ("UT", tt)], [("ps", bank)])
                P.copy('act', dst[:, tok], k.ps[bank], [("ps", bank)], [(dkey, qs)])
                sq_i = a.rings['sq'].next()
                P.actv('act', a.sq[sq_i], k.ps[bank], AF.Square, [("ps", bank)], [("a_sq", sq_i)])
                for cc in range(2):
                    nb_ = 2 + cc
                    P.mm(k.ps[nb_], a.sel[cc], a.sq[sq_i], True, True, ["a_sel", ("a_sq", sq_i)], [("ps", nb_)])
                    P.op('dve', lambda e, o_=a.mx[:, which, cc, tt:tt + 1], i_=k.ps[nb_]:
                         e.reduce_max(out=o_, in_=i_, axis=AX.X), [("ps", nb_)], ["a_mx"])
        P.op('dve', lambda e: e.reduce_max(out=sm[:, C_MQ:C_MQ + 2], in_=a.mx[:, 0], axis=AX.X), ["a_mx"], ["a_small"])
        P.op('dve', lambda e: e.reduce_max(out=sm[:, C_MK:C_MK + 2], in_=a.mx[:, 1], axis=AX.X), ["a_mx"], ["a_small"])
        P.tt('dve', sm[:, C_M:C_M + 2], sm[:, C_MQ:C_MQ + 2], sm[:, C_MK:C_MK + 2], ALU.mult, ["a_small"], ["a_small"])
        P.actv('act', sm[:, C_M:C_M + 2], sm[:, C_M:C_M + 2], AF.Sqrt, ["a_small"], ["a_small"])
        P.ts('dve', sm[:, C_BIAS:C_BIAS + 2], sm[:, C_M:C_M + 2], -0.125, a.cfar[:, h:h + 1], ALU.mult, ALU.add,
             ["a_small"], ["a_small"])
        for g in range(4):
            hst_i = a.rings['hst'].next()
            hst = a.hst[hst_i]
            for cc in range(2):
                prt = slice(cc * 64, (cc + 1) * 64)
                nkb = 4 * g + 4
                for kb in range(nkb):
                    jmin = max(0, kb - 4 * g)
                    cols = slice(jmin * 128, 512)
                    bank = a.rings['s'].next()
                    P.mm(k.ps[bank][:, cols], kT[prt, kb * 128:(kb + 1) * 128],
                         qT[prt, g * 512 + jmin * 128:(g + 1) * 512], True, True,
                         [("a_kT", qs), ("a_qT", qs)], [("ps", bank)])
                    p_i = a.rings['pT'].next()
                    pT = a.pT[p_i]
                    P.actv('act', pT[:, cols], k.ps[bank][:, cols], AF.Exp, [("ps", bank), "a_small"],
                           [("a_pT", p_i)], bias=sm[:, C_BIAS + cc:C_BIAS + cc + 1], scale=0.125)
                    for j in range(jmin, 4):
                        qb = 4 * g + j
                        if kb == qb or kb == qb - 1:
                            wh = 0 if kb == qb else 1
                            P.tt('dve', pT[:, j * 128:(j + 1) * 128], pT[:, j * 128:(j + 1) * 128],
                                 a.eb[:, h, wh, :], ALU.mult, [("a_pT", p_i), "a_eb"], [("a_pT", p_i)])
                    for j in range(jmin, 4):
                        qb = 4 * g + j
                        P.mm(k.ps[2 + j][:, 0:129], pT[:, j * 128:(j + 1) * 128], a.v[:, kb, h, 0:129],
                             kb == 0, kb == qb, [("a_pT", p_i), ("a_v", kb)], [("ps", 2 + j)])
                for j in range(4):
                    acc = k.ps[2 + j]
                    rr = sm[:, C_R + cc:C_R + cc + 1]
                    P.op('dve', lambda e, o_=rr, i_=acc[:, 128:129]: e.reciprocal(out=o_, in_=i_),
                         [("ps", 2 + j)], ["a_small"])
                    if cc == 0:
                        P.ts('dve', a.t0[:, j, :], acc[:, 0:128], rr, None, ALU.mult, None,
                             [("ps", 2 + j), "a_small"], [("a_t0", j)])
                    else:
                        P.tt('dve', rr, rr, sm[:, C_NEGLAM:C_NEGLAM + 1], ALU.mult, ["a_small"], ["a_small"])
                        o_i = a.rings['o'].next()
                        o = a.o[o_i]
                        P.stt('dve', o, acc[:, 0:128], rr, a.t0[:, j, :], ALU.mult, ALU.add,
                              [("ps", 2 + j), "a_small", ("a_t0", j)], [("a_o", o_i)])
                        ss = sm[:, C_SS:C_SS + 1]
                        P.actv('act', a.ob[o_i], o, AF.Square, [("a_o", o_i)], [("a_ob", o_i), "a_small"], accum=ss)
                        P.ts('dve', ss, ss, 1.0 / 128.0, LN_EPS, ALU.mult, ALU.add, ["a_small"], ["a_small"])
                        P.actv('act', ss, ss, AF.Sqrt, ["a_small"], ["a_small"])
                        P.op('dve', lambda e, o_=ss: e.reciprocal(out=o_, in_=o_), ["a_small"], ["a_small"])
                        P.stt('dve', a.ob[o_i], o, ss, a.ng[:, h * 128:(h + 1) * 128], ALU.mult, ALU.mult,
                              [("a_o", o_i), "a_small", "a_ng"], [("a_ob", o_i)])
                        P.tr(bf7[:, j * 128:(j + 1) * 128], a.ob[o_i], c.identb, [("a_ob", o_i), "identb"], [("ps", 7)])
            P.copy('act', hst, bf7[:, 0:512], [("ps", 7)], [("a_hst", hst_i)])
            P.dma(k.htd[:, h, g * 512:(g + 1) * 512], hst, reads=[("a_hst", hst_i)], writes=[("htd", g)],
                  key=("a_hst", hst_i))
    outproj_ln(k, b, sub, "dwout", is_last)


_CACHE = {}


def kernel(**inputs):
    in_maps = prep_inputs(inputs)
    if "nc" not in _CACHE:
        _CACHE["nc"] = build_program()[0]
    nc = _CACHE["nc"]
    res = run_bass_kernel_spmd(nc, in_maps, core_ids=list(range(NCORES)))
    out = np.concatenate([np.asarray(r["out"]) for r in res.results], axis=0)
    return out.astype(np.float32)
```

```python
import math
import numpy as np
import concourse.bass as bass
import concourse.mybir as mybir
from concourse.bass_utils import run_bass_kernel_spmd

F32 = mybir.dt.float32
BF16 = mybir.dt.bfloat16
AF = mybir.ActivationFunctionType
ALU = mybir.AluOpType
AX = mybir.AxisListType

NCORES = 8
B_PER = 2
S = 2048
D = 1024
DFF = 2816
NG_FF = 22
ALPHA = (2.0 * 2) ** 0.25
LN_EPS = 1e-5
EPS_LN = LN_EPS / (ALPHA * ALPHA)
EPOCH = 20000


class Prog:
    def __init__(self, nc):
        self.nc = nc
        self.ops = []
        self.last_writer = {}
        self.readers = {}
        self.arena_names = set()

    def fence_arena(self):
        self.op('pool', lambda e: e.memset(self.fence_ap, 0.0), (), ["ARENA", "fence_ap"])

    def op(self, eng, fn, reads=(), writes=(), dma_key=None, final=False):
        reads = list(reads)
        writes = list(writes)
        for kk in reads + writes:
            nm = kk if isinstance(kk, str) else kk[0]
            if nm in self.arena_names:
                reads.append("ARENA")
                break
        deps = set()
        for k in reads:
            w = self.last_writer.get(k)
            if w is not None:
                deps.add(w)
            if not isinstance(k, str) and k[0] == "ps":
                deps.update(r for r in self.readers.get(k, ()) if self.ops[r]['eng'] != eng)
        for k in writes:
            w = self.last_writer.get(k)
            if w is not None:
                deps.add(w)
            deps.update(self.readers.get(k, ()))
        idx = len(self.ops)
        self.ops.append(dict(eng=eng, fn=fn, deps=deps, dma_key=dma_key, has_dep=(final or dma_key is not None), final=final))
        for k in reads:
            self.readers.setdefault(k, []).append(idx)
        for k in writes:
            self.last_writer[k] = idx
            self.readers[k] = []
        return idx

    def pe(self, fn, reads=(), writes=()):
        return self.op('pe', fn, reads, writes)

    def act(self, fn, reads=(), writes=()):
        return self.op('act', fn, reads, writes)

    def dve(self, fn, reads=(), writes=()):
        return self.op('dve', fn, reads, writes)

    def pool(self, fn, reads=(), writes=()):
        return self.op('pool', fn, reads, writes)

    def dma(self, out, in_, reads=(), writes=(), key=None, q='sp', final=False):
        assert key is not None
        return self.op(q, lambda e: e.dma_start(out=out, in_=in_), reads, writes, dma_key=key, final=final)

    def mm(self, out, lhsT, rhs, start, stop, reads, writes):
        return self.pe(lambda e: e.matmul(out, lhsT, rhs, start=start, stop=stop), reads, writes)

    def tr(self, out, in_, ident, reads, writes):
        return self.pe(lambda e: e.transpose(out, in_, ident), reads, writes)

    def actv(self, eng, out, in_, func, reads, writes, bias=None, scale=None, accum=None):
        kw = {}
        if bias is not None:
            kw['bias'] = bias
        if scale is not None:
            kw['scale'] = scale
        if accum is not None:
            kw['accum_out'] = accum
        return self.op(eng, lambda e: e.activation(out=out, in_=in_, func=func, **kw), reads, writes)

    def tt(self, eng, out, in0, in1, op, reads, writes):
        return self.op(eng, lambda e: e.tensor_tensor(out=out, in0=in0, in1=in1, op=op), reads, writes)

    def ts(self, eng, out, in0, s1, s2, op0, op1, reads, writes):
        if s2 is None:
            return self.op(eng, lambda e: e.tensor_scalar(out=out, in0=in0, scalar1=s1, scalar2=None, op0=op0),
                           reads, writes)
        return self.op(eng, lambda e: e.tensor_scalar(out=out, in0=in0, scalar1=s1, scalar2=s2, op0=op0, op1=op1),
                       reads, writes)

    def stt(self, eng, out, in0, scalar, in1, op0, op1, reads, writes):
        return self.op(eng, lambda e: e.scalar_tensor_tensor(out=out, in0=in0, scalar=scalar, in1=in1,
                                                             op0=op0, op1=op1), reads, writes)

    def copy(self, eng, out, in_, reads, writes):
        if eng == 'act':
            return self.op(eng, lambda e: e.copy(out=out, in_=in_), reads, writes)
        return self.op(eng, lambda e: e.tensor_copy(out=out, in_=in_), reads, writes)

    def memset(self, eng, ap, val, writes):
        return self.op(eng, lambda e: e.memset(ap, val), (), writes)

    def emit(self, final_keys=()):
        nc = self.nc
        ops = self.ops

        def skip(dop, o):
            return dop['eng'] == 'pe' and o['eng'] == 'pe' and dop['dma_key'] is None

        for o in ops:
            for d in o['deps']:
                if not skip(ops[d], o):
                    ops[d]['has_dep'] = True
        final_ops = set()
        for k in final_keys:
            w = self.last_writer.get(k)
            if w is not None:
                ops[w]['has_dep'] = True
                final_ops.add(w)
        engs = ['pe', 'act', 'dve', 'pool', 'sp']
        eng_cnt = {e: 0 for e in engs}
        eng_sems = {e: [] for e in engs}
        dma_sems = {}
        dma_cnt = {}
        nsem = [0]

        def new_sem():
            nsem[0] += 1
            return nc.alloc_semaphore("s%d" % nsem[0])

        for o in ops:
            o['ticket'] = None
            if not o['has_dep']:
                continue
            if o['dma_key'] is not None:
                k = o['dma_key']
                if k not in dma_sems:
                    dma_sems[k] = new_sem()
                    dma_cnt[k] = 0
                dma_cnt[k] += 16
                o['ticket'] = (dma_sems[k], dma_cnt[k], ('d', k))
            else:
                e = o['eng']
                c = eng_cnt[e]
                ep = c // EPOCH
                if ep >= len(eng_sems[e]):
                    eng_sems[e].append(new_sem())
                eng_cnt[e] = c + 1
                o['ticket'] = (eng_sems[e][ep], c - ep * EPOCH + 1, (e, ep))
        for o in ops:
            if o['final']:
                sem, val, sid = o['ticket']
                o['ticket'] = (sem, dma_cnt[o['dma_key']], sid)
        self.nsems = nsem[0]
        per_eng = {e: [] for e in engs}
        for i, o in enumerate(ops):
            per_eng[o['eng']].append(i)
        final_waits = [ops[w]['ticket'] for w in sorted(final_ops)]

        def run_engine(ename):
            def body(eng):
                seen = {}
                for i in per_eng[ename]:
                    o = ops[i]
                    need = {}
                    for d in o['deps']:
                        dop = ops[d]
                        if skip(dop, o):
                            continue
                        sem, val, sid = dop['ticket']
                        if seen.get(sid, 0) >= val:
                            continue
                        if sid not in need or need[sid][1] < val:
                            need[sid] = (sem, val)
                    for sid, (sem, val) in need.items():
                        eng.wait_ge(sem, val)
                        seen[sid] = val
                    if getattr(self, "trace", None) is not None:
                        self.trace.append((ename, i, [(sid, v) for sid, (s_, v) in need.items()],
                                           None if o['ticket'] is None else o['ticket'][1:]))
                    ins = o['fn'](eng)
                    if o['ticket'] is not None:
                        ins.then_inc(o['ticket'][0], 16 if o['dma_key'] is not None else 1)
                if ename == 'sp':
                    for (sem, val, sid) in final_waits:
                        if seen.get(sid, 0) < val:
                            eng.wait_ge(sem, val)
                            seen[sid] = val
            return body

        with nc.Block() as block:
            block.tensor(run_engine('pe'))
            block.scalar(run_engine('act'))
            block.vector(run_engine('dve'))
            block.gpsimd(run_engine('pool'))
            block.sync(run_engine('sp'))


class Ring:
    def __init__(self, n):
        self.n = n
        self.i = -1

    def next(self):
        self.i = (self.i + 1) % self.n
        return self.i


def grp(W, wc):
    K, N = W.shape
    return np.ascontiguousarray(W.reshape(K // 128, 128, N // wc, wc).transpose(2, 1, 0, 3))


def featmajor(v):
    sh = v.shape
    n = sh[-1] // 128
    a = v.reshape(sh[:-1] + (n, 128))
    return np.ascontiguousarray(np.moveaxis(a, -1, 0))


def t5_bucket_np(rel):
    nb = 16
    max_exact = 8
    n = -rel
    ret = np.where(n < 0, nb, 0)
    n = np.abs(n)
    nf = np.maximum(n, 1).astype(np.float32)
    large = max_exact + (np.log(nf / max_exact) / math.log(128 / max_exact) * (nb - max_exact)).astype(np.int32)
    large = np.minimum(large, nb - 1)
    return ret + np.where(n < max_exact, n, large)


def weight_shapes():
    return {
        "wqk": [16, 128, 8, 128],
        "wv_m": [4, 128, 8, 256],
        "wo_m": [4, 128, 8, 256],
        "wg_m": [1, 128, 8, 8],
        "wout_m": [8, 128, 8, 128],
        "dq": [8, 128, 8, 128],
        "dk": [8, 128, 8, 128],
        "dv": [2, 128, 8, 512],
        "dwout": [8, 128, 8, 128],
        "wup0": [22, 128, 8, 256],
        "wup1": [22, 128, 8, 256],
        "wdown0": [8, 128, 22, 128],
        "wdown1": [8, 128, 22, 128],
    }


def small_shapes():
    return {
        "cT": [128, 8, 2],
        "b_adaT": [128, 2, 48],
        "lnT": [128, 2, 4, 8],
        "cwm": [128, 16, 5],
        "cwf": [128, 2, 44, 4],
        "normg": [2, 1024],
        "bgate": [1, 8],
        "dlam": [1, 256],
        "biasT": [128, 8, 2, 128],
        "cfar": [1, 8],
        "ident": [128, 128],
        "tri": [128, 128],
    }


def prep_inputs(inputs):
    f = lambda a: np.ascontiguousarray(np.asarray(a, dtype=np.float32))
    x = f(inputs["x"])
    c = f(inputs["c"])
    shared = {}
    m_w_in = f(inputs["m_w_in"])[0]
    shared["wqk"] = grp(m_w_in[:, 0:2048], 128)
    shared["wv_m"] = grp(m_w_in[:, 2048:3072], 256)
    shared["wo_m"] = grp(m_w_in[:, 3072:4096], 256)
    shared["wg_m"] = grp(m_w_in[:, 4096:4104], 8)
    shared["wout_m"] = grp(f(inputs["m_w_out"])[0], 128)
    d_w_in = f(inputs["d_w_in"])[0]
    shared["dq"] = grp(d_w_in[:, 0:1024], 128)
    shared["dk"] = grp(d_w_in[:, 1024:2048], 128)
    shared["dv"] = grp(d_w_in[:, 2048:3072], 512)
    shared["dwout"] = grp(f(inputs["d_w_out"])[0], 128)
    f_w_up = f(inputs["f_w_up"])
    f_conv_w = f(inputs["f_conv_w"])
    f_conv_b = f(inputs["f_conv_b"])
    perm = np.concatenate([np.concatenate([np.arange(128 * g, 128 * g + 128),
                                           DFF + np.arange(128 * g, 128 * g + 128)]) for g in range(NG_FF)])
    cwf = np.zeros((128, 2, 44, 4), np.float32)
    for l in range(2):
        shared["wup%d" % l] = grp(f_w_up[l][:, perm], 256)
        shared["wdown%d" % l] = grp(f(inputs["f_w_down"])[l], 128)
        cw = np.concatenate([f_conv_w[l][:, perm], f_conv_b[l][None, perm]], axis=0)
        cwf[:, l] = featmajor(cw.T.copy().T).transpose(0, 2, 1) if False else \
            np.ascontiguousarray(cw.reshape(4, 44, 128).transpose(2, 1, 0))
    shared["cwf"] = cwf
    mcw = np.concatenate([f(inputs["m_conv_w"])[0], f(inputs["m_conv_b"])[0][None]], axis=0)
    shared["cwm"] = np.ascontiguousarray(mcw.reshape(5, 16, 128).transpose(2, 1, 0))
    shared["w_ada"] = f(inputs["w_ada"])
    shared["b_adaT"] = np.ascontiguousarray(f(inputs["b_ada"]).reshape(2, 48, 128).transpose(2, 0, 1))
    lg = f(inputs["ln_g"]).reshape(4, 8, 128)
    lb = f(inputs["ln_b"]).reshape(4, 8, 128)
    shared["lnT"] = np.ascontiguousarray(np.stack([lg, lb], 0).transpose(3, 0, 1, 2))
    shared["normg"] = np.stack([f(inputs["m_norm_g"])[0], f(inputs["d_norm_g"])[0]], 0)
    shared["bgate"] = f(inputs["m_b_gate"]).reshape(1, 8)
    shared["dlam"] = f(inputs["d_lambda"]).reshape(1, 256)
    rb = f(inputs["rel_bias"])
    kk = np.arange(128)[:, None]
    qq = np.arange(128)[None, :]
    b0 = t5_bucket_np(kk - qq)
    b1 = t5_bucket_np(kk - qq - 128)
    bt = np.stack([rb[b0], rb[b1]], 0)
    shared["biasT"] = np.ascontiguousarray(bt.transpose(1, 3, 0, 2))
    shared["cfar"] = np.ascontiguousarray(rb[15:16, :])
    shared["ident"] = np.eye(128, dtype=np.float32)
    shared["tri"] = np.triu(np.ones((128, 128), np.float32))
    in_maps = []
    for core in range(NCORES):
        m = dict(shared)
        m["x"] = np.ascontiguousarray(x[core * B_PER:(core + 1) * B_PER])
        cc = c[core * B_PER:(core + 1) * B_PER]
        m["cT"] = np.ascontiguousarray(cc.reshape(2, 8, 128).transpose(2, 1, 0))
        in_maps.append(m)
    return in_maps


class K:
    pass


def build_program(subs=((0, 0), (0, 1), (1, 0), (1, 1)), nseq=B_PER, debug_out=None, stop_after=None, skip_conv=False):
    nc = bass.Bass("TRN2", target_bir_lowering=False)
    P = Prog(nc)
    k = K()
    k.nc, k.P = nc, P
    dr = {}
    dr["x"] = nc.dram_tensor("x", [B_PER, S, D], F32, kind="ExternalInput").ap()
    dr["w_ada"] = nc.dram_tensor("w_ada", [2, D, 6 * D], F32, kind="ExternalInput").ap()
    for n, sh in weight_shapes().items():
        dr[n] = nc.dram_tensor(n, sh, F32, kind="ExternalInput").ap()
        dr[n + "_b"] = nc.dram_tensor(n + "_b", sh, BF16, kind="Internal").ap()
    for n, sh in small_shapes().items():
        dr[n] = nc.dram_tensor(n, sh, F32, kind="ExternalInput").ap()
    dr["out"] = nc.dram_tensor("out", [B_PER, S, D], F32, kind="ExternalOutput").ap()
    dr["xres"] = nc.dram_tensor("xres", [B_PER, 128, 8, S], F32, kind="Internal").ap()
    k.dr = dr

    def sb(name, shape, dt=F32):
        return nc.alloc_sbuf_tensor("s_" + name, shape, dt).ap()

    k.sb = sb
    k.ps = [nc.alloc_psum_tensor("ps%d" % i, [128, 512], F32).ap() for i in range(8)]

    def finish_early():
        P.emit(final_keys=[kk for kk in P.last_writer if not isinstance(kk, str) or True])
        return nc, k

    def convert(names):
        for n in names:
            src, dst = dr[n], dr[n + "_b"]
            for g in range(src.shape[0]):
                P.dma(dst[g], src[g], reads=(), writes=[(n + "_b", g)], key=("cv", n), q='pool', final=True)

    early = ["wqk", "wv_m", "wo_m", "wg_m", "wout_m", "wup0", "wdown0"]
    late = [n for n in weight_shapes() if n not in early]
    if not skip_conv:
        convert(early)

    if stop_after == 'conv':
        return finish_early()
    c = K()
    k.c = c
    c.ident = sb("ident", [128, 128])
    c.identb = sb("identb", [128, 128], BF16)
    c.tri = sb("tri", [128, 128])
    c.onesb = sb("onesb", [128, 128], BF16)
    c.ones = sb("ones", [128, 128])
    c.lnT = sb("lnT", [128, 2, 4, 8])
    c.cwm = sb("cwm", [128, 16, 5])
    c.cwf = sb("cwf", [128, 2, 44, 4])
    c.cT = sb("cT", [128, 8, 2])
    c.badaT = sb("badaT", [128, 2, 48])
    P.dma(c.ident, dr["ident"], writes=["ident"], key="c0")
    P.dma(c.tri, dr["tri"], writes=["tri"], key="c1")
    P.dma(c.lnT, dr["lnT"], writes=["lnT"], key="c2")
    P.dma(c.cwm, dr["cwm"], writes=["cwm"], key="c3")
    P.dma(c.cwf, dr["cwf"], writes=["cwf"], key="c4")
    P.dma(c.cT, dr["cT"], writes=["cT"], key="c5")
    P.dma(c.badaT, dr["b_adaT"], writes=["badaT"], key="c6")
    P.copy('dve', c.identb, c.ident, ["ident"], ["identb"])
    P.memset('dve', c.onesb, 1.0 / 1024.0, ["onesb"])
    P.memset('dve', c.ones, 1.0, ["ones"])

    k.UT = sb("UT", [128, 8, S], BF16)
    k.htd = nc.dram_tensor("htd", [128, 8, S], BF16, kind="Internal").ap()
    k.htt = [sb("htt%d" % i, [128, 8, 512], BF16) for i in range(1)]
    k.htt_ring = Ring(1)
    k.xt = [sb("xt%d" % i, [128, 8, 512]) for i in range(1)]
    k.xt_ring = Ring(1)
    k.zb = sb("zb", [128, 8, 512], BF16)
    k.zq = sb("zq", [128, 8, 512], BF16)
    k.msq = sb("msq", [128, 512])
    k.rstd = sb("rstd", [128, 512])
    k.mean = sb("mean", [128, 512])
    k.xo = [sb("xo%d" % i, [128, 8, 512]) for i in range(1)]
    k.xo_ring = Ring(1)
    k.otok = [sb("otok%d" % i, [128, 1024]) for i in range(2)]
    k.otok_ring = Ring(2)
    k.wd = [sb("wd%d" % i, [128, 22, 128], BF16) for i in range(2)]
    k.wd_ring = Ring(2)
    P.fence_ap = sb("fence", [128, 8])
    m = K()
    k.m = m
    m.sh = sb("m_sh", [128, 4, 2, 8])
    m.sc1 = sb("m_sc1", [128, 4, 2, 8])
    m.gt = sb("m_gt", [128, 4, 2, 8])
    m.g1 = sb("m_g1", [128, 4, 2, 8])
    m.b1 = sb("m_b1", [128, 4, 2, 8])
    ada = sb("ada", [128, 2, 48, 2])
    csT = sb("csT", [128, 8, 2])
    k.arena_base = nc.sbuf_base
    print("arena base", k.arena_base, "avail", nc.sbuf_top - nc.sbuf_base)
    k.arena_sets = {}

    P.actv('act', csT, c.cT, AF.Silu, ["cT"], ["csT"])
    k.arena_cur = "startup"
    P.arena_names.update(["wa"])
    wa = [sb("wa%d" % i, [128, 8, 768]) for i in range(2)]
    nc.sbuf_base = k.arena_base
    war = Ring(2)
    wada_v = dr["w_ada"].rearrange("l (kc p) j -> l p kc j", p=128)
    for l in range(2):
        for jg in range(8):
            s_ = war.next()
            P.dma(wa[s_], wada_v[l][:, :, jg * 768:(jg + 1) * 768], writes=[("wa", s_)], key=("wa", s_))
            for jc in range(6):
                j = jg * 6 + jc
                pa = k.ps[j % 2][:, 0:2]
                for kc in range(8):
                    P.mm(pa, wa[s_][:, kc, jc * 128:(jc + 1) * 128], csT[:, kc, :], kc == 0, kc == 7,
                         [("wa", s_), "csT"], [("ps", j % 2)])
                P.ts('dve', ada[:, l, j, :], pa, c.badaT[:, l, j:j + 1], None, ALU.add, None,
                     [("ps", j % 2), "badaT"], ["ada"])
    for l in range(2):
        for s_ in range(2):
            sub = 2 * l + s_
            for b in range(2):
                base = s_ * 24
                P.copy('dve', m.sh[:, sub, b, :], ada[:, l, base:base + 8, b], ["ada"], ["mod"])
                P.ts('dve', m.sc1[:, sub, b, :], ada[:, l, base + 8:base + 16, b], 1.0, None, ALU.add, None,
                     ["ada"], ["mod"])
                P.ts('dve', m.gt[:, sub, b, :], ada[:, l, base + 16:base + 24, b], 1.0, 1.0 / ALPHA,
                     ALU.add, ALU.mult, ["ada"], ["mod"])
    for sub in range(3):
        for b in range(2):
            P.tt('dve', m.g1[:, sub, b, :], c.lnT[:, 0, sub, :], m.sc1[:, sub + 1, b, :], ALU.mult,
                 ["lnT", "mod"], ["mod2"])
            P.tt('dve', m.b1[:, sub, b, :], c.lnT[:, 1, sub, :], m.sc1[:, sub + 1, b, :], ALU.mult,
                 ["lnT", "mod"], ["mod2"])
            P.tt('dve', m.b1[:, sub, b, :], m.b1[:, sub, b, :], m.sh[:, sub + 1, b, :], ALU.add,
                 ["mod", "mod2"], ["mod2"])

    if stop_after == 'ada':
        return finish_early()
    first_sub = 2 * subs[0][0] + subs[0][1]
    last_sub = 2 * subs[-1][0] + subs[-1][1]
    for b in range(nseq):
        load_input(k, b, first_sub)
        if b == 0 and not skip_conv:
            convert(late)
        if stop_after == 'load':
            return finish_early()
        for (l, s_) in subs:
            sub = 2 * l + s_
            if s_ == 1:
                ffn_sublayer(k, l, b, sub, is_last=(sub == last_sub))
            elif l == 0:
                mlstm_sublayer(k, b, sub, is_last=(sub == last_sub))
            else:
                attn_sublayer(k, b, sub, is_last=(sub == last_sub))
    P.emit(final_keys=[("out", b, t) for b in range(nseq) for t in range(16)])
    return nc, k


def arena_set(k, name, alloc_fn):
    if k.arena_cur != name:
        k.P.fence_arena()
        k.arena_cur = name
    if name not in k.arena_sets:
        k.nc.sbuf_base = k.arena_base
        k.arena_sets[name] = alloc_fn()
    return k.arena_sets[name]


def load_input(k, b, sub):
    import os
    P, c, m = k.P, k.c, k.m
    dbg = os.environ.get("DBG", "")
    for tb in range(int(os.environ.get("NTB", "16"))):
        s_ = k.otok_ring.next()
        xin = k.otok[s_]
        P.dma(xin, k.dr["x"][b, tb * 128:(tb + 1) * 128, :], reads=(), writes=[("otok", s_)], key=("otok", s_))
        xs = k.xo_ring.next()
        xo = k.xo[xs]
        for half in range(2):
            bank = 6 + half
            for j in range(4):
                dc = half * 4 + j
                P.tr(k.ps[bank][:, j * 128:(j + 1) * 128], xin[:, dc * 128:(dc + 1) * 128], c.ident,
                     [("otok", s_), "ident"], [("ps", bank)])
            if "noact" not in dbg:
                P.copy('act', xo[:, half * 4:half * 4 + 4, 0:128],
                       k.ps[bank].rearrange("p (a b) -> p a b", a=4), [("ps", bank)], [("xo", xs)])
            for j in range(4):
                if "nodve" in dbg:
                    break
                dc = half * 4 + j
                P.ts('dve', k.UT[:, dc, tb * 128:(tb + 1) * 128], k.ps[bank][:, j * 128:(j + 1) * 128],
                     m.sc1[:, sub, b, dc:dc + 1], m.sh[:, sub, b, dc:dc + 1], ALU.mult, ALU.add,
                     [("ps", bank), "mod"], [("UT", tb // 4)])
        if "nost" not in dbg:
            P.dma(k.dr["xres"][b][:, :, tb * 128:(tb + 1) * 128], xo[:, :, 0:128], reads=[("xo", xs)],
                  writes=[("xres", b, tb // 4)], key=("xo", xs))


def ln_epilogue(k, b, sub, tt, ybank_fn, is_last, defer=False):
    P, c, m = k.P, k.c, k.m
    tok = slice(tt * 512, (tt + 1) * 512)
    zs = k.xt_ring.next()
    z = k.xt[zs]
    P.dma(z, k.dr["xres"][b][:, :, tok], reads=[("xres", b, tt)], writes=[("xt", zs)], key=("xt", zs))
    for dc in range(8):
        bank = ybank_fn(dc)
        P.stt('dve', z[:, dc, :], k.ps[bank], m.gt[:, sub, b, dc:dc + 1], z[:, dc, :], ALU.mult, ALU.add,
              [("ps", bank), "mod", ("xt", zs)], [("xt", zs)])

    def part_b():
        ln_part_b(k, b, sub, tt, z, zs, is_last)

    if defer:
        return part_b
    part_b()
    return None


def ln_part_b(k, b, sub, tt, z, zs, is_last):
    P, c, m = k.P, k.c, k.m
    tok = slice(tt * 512, (tt + 1) * 512)
    P.copy('act', k.zb, z, [("xt", zs)], ["zb"])
    P.actv('act', k.zq, z, AF.Square, [("xt", zs)], ["zq"])
    for dc in range(8):
        P.mm(k.ps[6], c.onesb, k.zb[:, dc, :], dc == 0, dc == 7, ["onesb", "zb"], [("ps", 6)])
    for dc in range(8):
        P.mm(k.ps[7], c.onesb, k.zq[:, dc, :], dc == 0, dc == 7, ["onesb", "zq"], [("ps", 7)])
    P.copy('act', k.mean, k.ps[6], [("ps", 6)], ["mean"])
    P.tt('dve', k.msq, k.mean, k.mean, ALU.mult, ["mean"], ["msq"])
    P.tt('dve', k.rstd, k.ps[7], k.msq, ALU.subtract, [("ps", 7), "msq"], ["rstd"])
    P.ts('dve', k.rstd, k.rstd, EPS_LN, None, ALU.add, None, ["rstd"], ["rstd"])
    P.actv('act', k.rstd, k.rstd, AF.Sqrt, ["rstd"], ["rstd"])
    P.op('dve', lambda e: e.reciprocal(out=k.rstd, in_=k.rstd), ["rstd"], ["rstd"])
    P.tt('dve', z, z, k.mean.unsqueeze(1).to_broadcast([128, 8, 512]), ALU.subtract,
         [("xt", zs), "mean"], [("xt", zs)])
    P.tt('pool', z, z, k.rstd.unsqueeze(1).to_broadcast([128, 8, 512]), ALU.mult,
         [("xt", zs), "rstd"], [("xt", zs)])
    xs = k.xo_ring.next()
    xo = k.xo[xs]
    for dc in range(8):
        P.actv('act', xo[:, dc, :], z[:, dc, :], AF.Identity, [("xt", zs), "lnT"], [("xo", xs)],
               bias=c.lnT[:, 1, sub, dc:dc + 1], scale=c.lnT[:, 0, sub, dc:dc + 1])
    if not is_last:
        for dc in range(8):
            P.ts('dve', k.UT[:, dc, tok], z[:, dc, :], m.g1[:, sub, b, dc:dc + 1], m.b1[:, sub, b, dc:dc + 1],
                 ALU.mult, ALU.add, [("xt", zs), "mod2"], [("UT", tt)])
        P.dma(k.dr["xres"][b][:, :, tok], xo, reads=[("xo", xs)], writes=[("xres", b, tt)], key=("xo", xs))
    else:
        for j4 in range(4):
            tb = tt * 4 + j4
            os_ = k.otok_ring.next()
            ot = k.otok[os_]
            for half in range(2):
                bank = 6 + half
                for j in range(4):
                    dc = half * 4 + j
                    P.tr(k.ps[bank][:, j * 128:(j + 1) * 128], xo[:, dc, j4 * 128:(j4 + 1) * 128], c.ident,
                         [("xo", xs), "ident"], [("ps", bank)])
                P.copy('act' if half == 0 else 'dve', ot[:, half * 512:(half + 1) * 512], k.ps[bank],
                       [("ps", bank)], [("otok", os_)])
            P.dma(k.dr["out"][b, tb * 128:(tb + 1) * 128, :], ot, reads=[("otok", os_)],
                  writes=[("out", b, tb)], key=("otok", os_))


def outproj_ln(k, b, sub, wname, is_last):
    P = k.P
    W = k.dr[wname + "_b"]
    yr = Ring(2)
    for tt in range(4):
        tok = slice(tt * 512, (tt + 1) * 512)
        hs_ = k.htt_ring.next()
        ht = k.htt[hs_]
        P.dma(ht, k.htd[:, :, tok], reads=[("htd", tt)], writes=[("htt", hs_)], key=("htt", hs_))

        def ybank(dc, tt=tt, ht=ht, hs_=hs_):
            ws = k.wd_ring.next()
            wt = k.wd[ws]
            P.dma(wt[:, 0:8, :], W[dc], reads=[(wname + "_b", dc)], writes=[("wd", ws)], key=("wd", ws))
            bank = 4 + yr.next()
            for kc in range(8):
                P.mm(k.ps[bank], wt[:, kc, :], ht[:, kc, :], kc == 0, kc == 7,
                     [("wd", ws), ("htt", hs_)], [("ps", bank)])
            return bank

        ln_epilogue(k, b, sub, tt, ybank, is_last)


def ffn_sublayer(k, l, b, sub, is_last):
    P, c = k.P, k.c
    def alloc():
        f = K()
        k.P.arena_names.update(["wup", "G", "hs", "acc", "halo"])
        f.wup = [k.sb("wup%d" % i, [128, 8, 256], BF16) for i in range(4)]
        f.wup_ring = Ring(4)
        f.G = k.sb("G", [128, NG_FF, 512], BF16)
        f.hs = [k.sb("hs%d" % i, [128, 514]) for i in range(8)]
        f.hs_ring = Ring(8)
        f.acc = [k.sb("acc%d" % i, [128, 512]) for i in range(8)]
        f.acc_ring = Ring(8)
        f.halo = k.sb("halo", [128, 44, 2])
        f.up_ring = Ring(4)
        return f

    f = arena_set(k, "ffn", alloc)
    WU = k.dr["wup%d_b" % l]
    WD = k.dr["wdown%d_b" % l]
    pending_b = None
    for tt in range(4):
        tok = slice(tt * 512, (tt + 1) * 512)
        for g in range(NG_FF):
            if g == 3 and pending_b is not None:
                pending_b()
                pending_b = None
            ws = f.wup_ring.next()
            wt = f.wup[ws]
            P.dma(wt, WU[g], reads=[("wup%d_b" % l, g)], writes=[("wup", ws)], key=("wup", ws))
            accs = []
            for half in range(2):
                idx = 2 * g + half
                bank = f.up_ring.next()
                pb = k.ps[bank]
                for kc in range(8):
                    P.mm(pb, wt[:, kc, half * 128:(half + 1) * 128], k.UT[:, kc, tok], kc == 0, kc == 7,
                         [("wup", ws), ("UT", tt)], [("ps", bank)])
                hs_i = f.hs_ring.next()
                hs = f.hs[hs_i]
                ac_i = f.acc_ring.next()
                acc = f.acc[ac_i]
                cw = c.cwf[:, l, idx, :]
                P.copy('act', hs[:, 2:514], pb, [("ps", bank)], [("hs", hs_i)])
                if tt == 0:
                    P.memset('pool', hs[:, 0:2], 0.0, [("hs", hs_i)])
                else:
                    P.copy('pool', hs[:, 0:2], f.halo[:, idx, :], [("halo", idx)], [("hs", hs_i)])
                P.actv('act', acc, pb, AF.Identity, [("ps", bank), "cwf"], [("acc", ac_i)],
                       bias=cw[:, 3:4], scale=cw[:, 2:3])
                P.stt('dve', acc, hs[:, 1:513], cw[:, 1:2], acc, ALU.mult, ALU.add,
                      [("hs", hs_i), "cwf", ("acc", ac_i)], [("acc", ac_i)])
                P.stt('dve', acc, hs[:, 0:512], cw[:, 0:1], acc, ALU.mult, ALU.add,
                      [("hs", hs_i), "cwf", ("acc", ac_i)], [("acc", ac_i)])
                if tt < 3:
                    P.copy('pool', f.halo[:, idx, :], hs[:, 512:514], [("hs", hs_i)], [("halo", idx)])
                if half == 1:
                    P.actv('act', acc, acc, AF.Gelu, [("acc", ac_i)], [("acc", ac_i)])
                accs.append((acc, ac_i))
            P.tt('pool', f.G[:, g, :], accs[0][0], accs[1][0], ALU.mult,
                 [("acc", accs[0][1]), ("acc", accs[1][1])], [("G", g)])
        yr = Ring(2)

        def ybank(dc, tt=tt):
            ws = k.wd_ring.next()
            wt = k.wd[ws]
            P.dma(wt, WD[dc], reads=[("wdown%d_b" % l, dc)], writes=[("wd", ws)], key=("wd", ws))
            bank = 4 + yr.next()
            for kc in range(NG_FF):
                P.mm(k.ps[bank], wt[:, kc, :], f.G[:, kc, :], kc == 0, kc == NG_FF - 1,
                     [("wd", ws), ("G", kc)], [("ps", bank)])
            return bank

        pending_b = ln_epilogue(k, b, sub, tt, ybank, is_last, defer=(tt < 3))
    if pending_b is not None:
        pending_b()


def mlstm_sublayer(k, b, sub, is_last):
    P, c, nc = k.P, k.c, k.nc
    dr = k.dr

    def alloc():
        a = K()
        P.arena_names.update(["m_w", "m_wv", "m_wo", "m_wg", "m_hs", "m_acc", "m_qT", "m_kT", "m_v", "m_so",
                              "m_kt", "m_C", "m_Cb", "m_g", "m_aT", "m_kw", "m_h", "m_hb", "m_hst", "m_ng",
                              "m_sm", "m_junk", "m_den", "m_ss", "m_t1"])
        a.w = [k.sb("m_w%d" % i, [128, 8, 128], BF16) for i in range(2)]
        a.wv = k.sb("m_wv", [128, 8, 256], BF16)
        a.wo = k.sb("m_wo", [128, 8, 256], BF16)
        a.wg = k.sb("m_wg", [128, 8, 8], BF16)
        a.big = k.sb("m_big", [128, 4104])
        a.hs = a.big[:, 0:3 + S]
        a.acc = a.big[:, 2052:2052 + S]
        a.numall = a.big[:, 0:4096].rearrange("p (a b) -> p a b", a=16)
        a.den = k.sb("m_den", [128, 16])
        a.qT = k.sb("m_qT", [128, 2, S], BF16)
        a.kT = k.sb("m_kT", [128, 2, S], BF16)
        a.v = k.sb("m_v", [128, 16, 264], BF16)
        a.so = k.sb("m_so", [128, 16, 256], BF16)
        a.kt = k.sb("m_kt", [128, 16, 256], BF16)
        a.C = k.sb("m_C", [128, 2, 264])
        a.Cb = k.sb("m_Cb", [128, 2, 264], BF16)
        a.G = k.sb("m_G", [128, 16, 8])
        a.bg = k.sb("m_bg", [128, 8])
        a.sp = k.sb("m_sp", [128, 64])
        a.csp = k.sb("m_csp", [128, 64])
        a.bsp = k.sb("m_bsp", [128, 64])
        a.as_ = k.sb("m_as", [128, 64])
        a.amr = k.sb("m_amr", [128, 64])
        a.amc = k.sb("m_amc", [128, 2])
        a.dg = k.sb("m_dg", [128, 64])
        a.M = k.sb("m_M", [128, 17, 4])
        a.tmp = k.sb("m_tmp", [128, 64])
        a.es = k.sb("m_es", [128, 64])
        a.ws = k.sb("m_ws", [128, 64])
        a.dec = k.sb("m_dec", [128, 64])
        a.flo = k.sb("m_flo", [128, 64])
        a.aT = [k.sb("m_aT%d" % i, [128, 128], BF16) for i in range(2)]
        a.kw = [k.sb("m_kw%d" % i, [128, 256], BF16) for i in range(2)]
        a.junk = k.sb("m_junk", [128, 256], BF16)
        a.hst = [k.sb("m_hst%d" % i, [128, 2, 512], BF16) for i in range(2)]
        a.ng = k.sb("m_ng", [128, 1024])
        a.sm = k.sb("m_sm", [128, 4, 16])
        a.rings = dict(w=Ring(2), aT=Ring(2), kw=Ring(2), h=Ring(2), hst=Ring(2))
        return a

    a = arena_set(k, "mlstm", alloc)
    G_KEYS = ["m_g"]
    P.dma(a.ng, dr["normg"][0:1, :].partition_broadcast(128), writes=["m_ng"], key="m_c0")
    P.dma(a.bg, dr["bgate"].partition_broadcast(128), writes=["m_g"], key="m_c1")
    P.dma(a.wg, dr["wg_m_b"][0], reads=[("wg_m_b", 0)], writes=["m_wg"], key="m_c2")
    P.memset('pool', a.hs[:, 0:3], 0.0, ["m_hs"])
    P.memset('pool', a.v[:, :, 256:257], 1.0, ["m_v"])

    for cch in range(16):
        for kc in range(8):
            P.mm(k.ps[0][:, cch * 8:(cch + 1) * 8], k.UT[:, kc, cch * 128:(cch + 1) * 128], a.wg[:, kc, :],
                 kc == 0, kc == 7, [("UT", cch // 4), "m_wg"], [("ps", 0)])
    P.tt('dve', a.G, k.ps[0][:, 0:128].rearrange("p (a b) -> p a b", a=16),
         a.bg.unsqueeze(1).to_broadcast([128, 16, 8]), ALU.add, [("ps", 0), "m_g"], G_KEYS)
    sp3 = a.sp.rearrange("p (a b) -> p a b", a=16)
    as3 = a.as_.rearrange("p (a b) -> p a b", a=16)
    P.actv('act', sp3, a.G[:, :, 4:8], AF.Exp, G_KEYS, G_KEYS, scale=-1.0)
    P.actv('act', a.sp, a.sp, AF.Ln, G_KEYS, G_KEYS, bias=1.0)
    P.mm(k.ps[1][:, 0:64], c.tri, a.sp, True, True, ["tri"] + G_KEYS, [("ps", 1)])
    P.mm(k.ps[1][:, 64:128], c.ones, a.sp, True, True, ["ones"] + G_KEYS, [("ps", 1)])
    P.copy('dve', a.csp, k.ps[1][:, 0:64], [("ps", 1)], G_KEYS)
    P.copy('dve', a.bsp, k.ps[1][:, 64:128], [("ps", 1)], G_KEYS)
    P.tt('dve', as3, a.G[:, :, 0:4], a.csp.rearrange("p (a b) -> p a b", a=16), ALU.add, G_KEYS, G_KEYS)
    P.tr(k.ps[2][0:64, 0:128], a.as_, c.ident, G_KEYS + ["ident"], [("ps", 2)])
    P.op('dve', lambda e: e.reduce_max(out=a.amc[0:64, 0:1], in_=k.ps[2][0:64, 0:128], axis=AX.X),
         [("ps", 2)], G_KEYS)
    P.ts('dve', a.dg[0:64, :], c.ident[0:64, 0:64], a.amc[0:64, 0:1], None, ALU.mult, None, ["ident"] + G_KEYS, G_KEYS)
    P.mm(k.ps[2][:, 128:192], c.ones[0:64, :], a.dg[0:64, :], True, True, ["ones"] + G_KEYS, [("ps", 2)])
    P.copy('dve', a.amr, k.ps[2][:, 128:192], [("ps", 2)], G_KEYS)
    amr3 = a.amr.rearrange("p (a b) -> p a b", a=16)
    bsp3 = a.bsp.rearrange("p (a b) -> p a b", a=16)
    P.memset('dve', a.M[:, 0, :], 0.0, G_KEYS)
    for cch in range(16):
        P.tt('dve', a.tmp[:, 0:4], a.M[:, cch, :], amr3[:, cch, :], ALU.max, G_KEYS, G_KEYS)
        P.tt('dve', a.M[:, cch + 1, :], a.tmp[:, 0:4], bsp3[:, cch, :], ALU.subtract, G_KEYS, G_KEYS)
    mprev = a.M[:, 0:16, :]
    mnew = a.M[:, 1:17, :]
    tmp3 = a.tmp.rearrange("p (a b) -> p a b", a=16)
    es3 = a.es.rearrange("p (a b) -> p a b", a=16)
    ws3 = a.ws.rearrange("p (a b) -> p a b", a=16)
    dec3 = a.dec.rearrange("p (a b) -> p a b", a=16)
    flo3 = a.flo.rearrange("p (a b) -> p a b", a=16)
    csp3 = a.csp.rearrange("p (a b) -> p a b", a=16)
    P.tt('dve', tmp3, as3, mprev, ALU.subtract, G_KEYS, G_KEYS)
    P.actv('act', es3, tmp3, AF.Exp, G_KEYS, G_KEYS)
    P.tt('dve', tmp3, as3, bsp3, ALU.subtract, G_KEYS, G_KEYS)
    P.tt('dve', tmp3, tmp3, mnew, ALU.subtract, G_KEYS, G_KEYS)
    P.actv('act', ws3, tmp3, AF.Exp, G_KEYS, G_KEYS)
    P.tt('dve', tmp3, mprev, bsp3, ALU.subtract, G_KEYS, G_KEYS)
    P.tt('dve', tmp3, tmp3, mnew, ALU.subtract, G_KEYS, G_KEYS)
    P.actv('act', dec3, tmp3, AF.Exp, G_KEYS, G_KEYS)
    P.tt('dve', tmp3, csp3, mprev, ALU.subtract, G_KEYS, G_KEYS)
    P.actv('act', flo3, tmp3, AF.Exp, G_KEYS, G_KEYS, bias=math.log(16.0))

    bf7 = k.ps[7].bitcast(BF16)
    pr = Ring(2)
    for h in range(4):
        P.memset('pool', a.hs[:, 0:3], 0.0, ["m_hs"])
        for which, dst in ((0, a.qT), (1, a.kT)):
            for dc in range(2):
                chunk = which * 8 + 2 * h + dc
                ws_ = a.rings['w'].next()
                P.dma(a.w[ws_], dr["wqk_b"][chunk], reads=[("wqk_b", chunk)], writes=[("m_w", ws_)], key=("m_w", ws_))
                for tt in range(4):
                    bank = 4 + pr.next()
                    for kc in range(8):
                        P.mm(k.ps[bank], a.w[ws_][:, kc, :], k.UT[:, kc, tt * 512:(tt + 1) * 512], kc == 0, kc == 7,
                             [("m_w", ws_), ("UT", tt)], [("ps", bank)])
                    P.copy('act', a.hs[:, 3 + tt * 512:3 + (tt + 1) * 512], k.ps[bank], [("ps", bank)], ["m_hs"])
                cw = c.cwm[:, chunk, :]
                P.ts('dve', a.acc, a.hs[:, 3:3 + S], cw[:, 3:4], cw[:, 4:5], ALU.mult, ALU.add, ["m_hs", "cwm"], ["m_acc"])
                P.stt('dve', a.acc, a.hs[:, 2:2 + S], cw[:, 2:3], a.acc, ALU.mult, ALU.add, ["m_hs", "cwm", "m_acc"], ["m_acc"])
                P.stt('dve', a.acc, a.hs[:, 1:1 + S], cw[:, 1:2], a.acc, ALU.mult, ALU.add, ["m_hs", "cwm", "m_acc"], ["m_acc"])
                P.stt('dve', a.acc, a.hs[:, 0:S], cw[:, 0:1], a.acc, ALU.mult, ALU.add, ["m_hs", "cwm", "m_acc"], ["m_acc"])
                P.actv('act', dst[:, dc, :], a.acc, AF.Silu, ["m_acc"], ["m_qT" if which == 0 else "m_kT"])
        P.dma(a.wv, dr["wv_m_b"][h], reads=[("wv_m_b", h)], writes=["m_wv"], key="m_wv")
        P.dma(a.wo, dr["wo_m_b"][h], reads=[("wo_m_b", h)], writes=["m_wo"], key="m_wo")
        for wt, wkey, dst, dkey, fn in ((a.wv, "m_wv", a.v, "m_v", None), (a.wo, "m_wo", a.so, "m_so", AF.Sigmoid)):
            for c2 in range(8):
                bank = 4 + pr.next()
                for j in range(2):
                    cch = 2 * c2 + j
                    for kc in range(8):
                        P.mm(k.ps[bank][:, j * 256:(j + 1) * 256], k.UT[:, kc, cch * 128:(cch + 1) * 128], wt[:, kc, :],
                             kc == 0, kc == 7, [("UT", cch // 4), wkey], [("ps", bank)])
                src = k.ps[bank].rearrange("p (a b) -> p a b", a=2)
                if fn is None:
                    P.copy('dve', dst[:, 2 * c2:2 * c2 + 2, 0:256], src, [("ps", bank)], [dkey])
                else:
                    P.actv('act', dst[:, 2 * c2:2 * c2 + 2, :], src, fn, [("ps", bank)], [dkey])
        for c2 in range(8):
            for j in range(2):
                cch = 2 * c2 + j
                for dc in range(2):
                    P.tr(bf7[:, j * 256 + dc * 128:j * 256 + (dc + 1) * 128], a.kT[:, dc, cch * 128:(cch + 1) * 128],
                         c.identb, ["m_kT", "identb"], [("ps", 7)])
            P.copy('dve', a.kt[:, 2 * c2:2 * c2 + 2, :], bf7[:, 0:512].rearrange("p (a b) -> p a b", a=2),
                   [("ps", 7)], ["m_kt"])
        P.memset('pool', a.C, 0.0, ["m_C"])
        P.memset('pool', a.Cb, 0.0, ["m_Cb"])
        BIG = ["m_hs", "m_acc"]
        for cch in range(16):
            col = cch * 4 + h
            ctile = slice(cch * 128, (cch + 1) * 128)
            for dc in range(2):
                P.mm(k.ps[0][:, 0:128], a.kT[:, dc, ctile], a.qT[:, dc, ctile], dc == 0, dc == 1,
                     ["m_kT", "m_qT"], [("ps", 0)])
            ai = a.rings['aT'].next()
            P.stt('dve', a.aT[ai], k.ps[0][:, 0:128], a.es[:, col:col + 1], c.tri, ALU.mult, ALU.mult,
                  [("ps", 0), "tri"] + G_KEYS, [("m_aT", ai)])
            num = k.ps[1][:, 0:257]
            P.mm(num, a.aT[ai], a.v[:, cch, 0:257], True, cch == 0, [("m_aT", ai), "m_v"], [("ps", 1)])
            if cch > 0:
                for dc in range(2):
                    P.mm(num, a.qT[:, dc, ctile], a.Cb[:, dc, 0:257], False, dc == 1, ["m_qT", "m_Cb"], [("ps", 1)])
            P.copy('act', a.numall[:, cch, :], num[:, 0:256], [("ps", 1)], BIG)
            P.copy('dve', a.den[:, cch:cch + 1], num[:, 256:257], [("ps", 1)], ["m_den"])
            if cch < 15:
                ki = a.rings['kw'].next()
                P.ts('pool', a.kw[ki], a.kt[:, cch, :], a.ws[:, col:col + 1], None, ALU.mult, None,
                     ["m_kt"] + G_KEYS, [("m_kw", ki)])
                for dc in range(2):
                    P.mm(k.ps[2 + dc][:, 0:257], a.kw[ki][:, dc * 128:(dc + 1) * 128], a.v[:, cch, 0:257], True, True,
                         [("m_kw", ki), "m_v"], [("ps", 2 + dc)])
                    P.stt('dve', a.C[:, dc, 0:257], a.C[:, dc, 0:257], a.dec[:, col:col + 1], k.ps[2 + dc][:, 0:257],
                          ALU.mult, ALU.add, ["m_C", ("ps", 2 + dc)] + G_KEYS, ["m_C"])
                P.copy('act', a.Cb, a.C, ["m_C"], ["m_Cb"])
        sm = a.sm
        dab, ss, t1 = sm[:, 0, :], sm[:, 1, :], sm[:, 2, :]
        P.actv('act', dab, a.den, AF.Abs, ["m_den"], ["m_sm"])
        P.tt('dve', dab, dab, flo3[:, :, h], ALU.max, ["m_sm"] + G_KEYS, ["m_sm"])
        P.op('dve', lambda e, o_=dab: e.reciprocal(out=o_, in_=o_), ["m_sm"], ["m_sm"])
        for cch in range(16):
            P.actv('act', a.junk, a.numall[:, cch, :], AF.Square, BIG, ["m_junk", "m_ss"], accum=ss[:, cch:cch + 1])
        P.tt('dve', t1, dab, dab, ALU.mult, ["m_sm"], ["m_t1"])
        P.tt('dve', t1, t1, ss, ALU.mult, ["m_t1", "m_ss"], ["m_t1"])
        P.ts('dve', t1, t1, 1.0 / 256.0, LN_EPS, ALU.mult, ALU.add, ["m_t1"], ["m_t1"])
        P.actv('act', t1, t1, AF.Sqrt, ["m_t1"], ["m_t1"])
        P.op('dve', lambda e, o_=t1: e.reciprocal(out=o_, in_=o_), ["m_t1"], ["m_t1"])
        P.tt('dve', t1, t1, dab, ALU.mult, ["m_t1", "m_sm"], ["m_t1"])
        P.tt('dve', a.numall, a.numall, t1.unsqueeze(2).to_broadcast([128, 16, 256]), ALU.mult, BIG + ["m_t1"], BIG)
        P.tt('dve', a.numall, a.numall, a.ng[:, h * 256:(h + 1) * 256].unsqueeze(1).to_broadcast([128, 16, 256]),
             ALU.mult, BIG + ["m_ng"], BIG)
        P.tt('pool', a.kt, a.numall, a.so, ALU.mult, BIG + ["m_so"], ["m_kt"])
        for g4 in range(4):
            hst_i = a.rings['hst'].next()
            hst = a.hst[hst_i]
            for j in range(4):
                cch = g4 * 4 + j
                for dc in range(2):
                    P.tr(bf7[:, dc * 512 + j * 128:dc * 512 + (j + 1) * 128], a.kt[:, cch, dc * 128:(dc + 1) * 128],
                         c.identb, ["m_kt", "identb"], [("ps", 7)])
            P.copy('act', hst, bf7.rearrange("p (a b) -> p a b", a=2), [("ps", 7)], [("m_hst", hst_i)])
            P.dma(k.htd[:, 2 * h:2 * h + 2, g4 * 512:(g4 + 1) * 512], hst, reads=[("m_hst", hst_i)],
                  writes=[("htd", g4)], key=("m_hst", hst_i))
    outproj_ln(k, b, sub, "wout_m", is_last)


LAM_INIT1 = 0.8 - 0.6 * math.exp(-0.3 * 1)


def attn_sublayer(k, b, sub, is_last):
    P, c, nc = k.P, k.c, k.nc
    dr = k.dr

    def alloc():
        a = K()
        P.arena_names.update(["a_wq", "a_wk", "a_wv", "a_qT", "a_kT", "a_v", "a_sq", "a_pT", "a_t0", "a_o",
                              "a_hst", "a_small", "a_eb", "a_ng", "a_sel", "a_mx", "a_ev"])
        a.wq = [k.sb("a_wq%d" % i, [128, 8, 128], BF16) for i in range(1)]
        a.wk = [k.sb("a_wk%d" % i, [128, 8, 128], BF16) for i in range(1)]
        a.wv = k.sb("a_wv", [128, 8, 512], BF16)
        a.qT = [k.sb("a_qT%d" % i, [128, S], BF16) for i in range(1)]
        a.kT = [k.sb("a_kT%d" % i, [128, S], BF16) for i in range(1)]
        a.v = k.sb("a_v", [128, 16, 8, 130], BF16)
        a.sq = [k.sb("a_sq%d" % i, [128, 512], BF16) for i in range(2)]
        a.pT = [k.sb("a_pT%d" % i, [128, 512], BF16) for i in range(4)]
        a.t0 = k.sb("a_t0", [128, 4, 128])
        a.o = [k.sb("a_o%d" % i, [128, 128]) for i in range(2)]
        a.ob = [k.sb("a_ob%d" % i, [128, 128], BF16) for i in range(2)]
        a.hst = [k.sb("a_hst%d" % i, [128, 512], BF16) for i in range(2)]
        a.small = k.sb("a_small", [128, 64])
        a.ev = [k.sb("a_ev%d" % i, [128, 4, 130]) for i in range(2)]
        a.eb = k.sb("a_eb", [128, 8, 2, 128])
        a.ng = k.sb("a_ng", [128, 1024])
        a.lam = k.sb("a_lam", [128, 256])
        a.cfar = k.sb("a_cfar", [128, 8])
        a.sel = [k.sb("a_sel%d" % i, [128, 128], BF16) for i in range(2)]
        a.mx = k.sb("a_mx", [128, 2, 2, 4])
        a.rings = dict(w=Ring(1), qk=Ring(1), sq=Ring(2), pT=Ring(4), o=Ring(2), hst=Ring(2), s=Ring(2), ev=Ring(2))
        a.init = False
        return a

    a = arena_set(k, "attn", alloc)
    sm = a.small
    C_E1, C_E2, C_NEGLAM, C_BIAS, C_R, C_SS, C_MQ, C_MK, C_M = 0, 1, 2, 4, 8, 10, 12, 14, 16

    P.dma(a.lam, dr["dlam"].partition_broadcast(128), writes=["a_small"], key="a_c0")
    P.dma(a.cfar, dr["cfar"].partition_broadcast(128), writes=["a_small"], key="a_c1")
    P.dma(a.ng, dr["normg"][1:2, :].partition_broadcast(128), writes=["a_ng"], key="a_c2")
    P.dma(a.eb, dr["biasT"], writes=["a_eb"], key="a_c3")
    P.ts('dve', a.ng, a.ng, 1.0 - LAM_INIT1, None, ALU.mult, None, ["a_ng"], ["a_ng"])
    P.tt('dve', a.lam[:, 0:64], a.lam[:, 0:64], a.lam[:, 64:128], ALU.mult, ["a_small"], ["a_small"])
    P.tt('dve', a.lam[:, 128:192], a.lam[:, 128:192], a.lam[:, 192:256], ALU.mult, ["a_small"], ["a_small"])
    P.op('dve', lambda e: e.reduce_sum(out=sm[:, C_E1:C_E1 + 1], in_=a.lam[:, 0:64], axis=AX.X), ["a_small"], ["a_small"])
    P.op('dve', lambda e: e.reduce_sum(out=sm[:, C_E2:C_E2 + 1], in_=a.lam[:, 128:192], axis=AX.X), ["a_small"], ["a_small"])
    P.actv('act', sm[:, C_E1:C_E1 + 2], sm[:, C_E1:C_E1 + 2], AF.Exp, ["a_small"], ["a_small"])
    P.tt('dve', sm[:, C_NEGLAM:C_NEGLAM + 1], sm[:, C_E2:C_E2 + 1], sm[:, C_E1:C_E1 + 1], ALU.subtract,
         ["a_small"], ["a_small"])
    P.ts('dve', sm[:, C_NEGLAM:C_NEGLAM + 1], sm[:, C_NEGLAM:C_NEGLAM + 1], -LAM_INIT1, None, ALU.add, None,
         ["a_small"], ["a_small"])
    P.ts('dve', a.cfar, a.cfar, -1.0, None, ALU.mult, None, ["a_small"], ["a_small"])
    for h in range(8):
        P.actv('act', a.eb[:, h], a.eb[:, h], AF.Exp, ["a_eb", "a_small"], ["a_eb"], bias=a.cfar[:, h:h + 1])
    P.memset('pool', a.eb[64:128, :, 0, 0:64], 0.0, ["a_eb"])
    P.ts('dve', a.cfar, a.cfar, -1.0, None, ALU.mult, None, ["a_small"], ["a_small"])
    P.memset('pool', a.sel[0], 0.0, ["a_sel"])
    P.memset('pool', a.sel[1], 0.0, ["a_sel"])
    P.memset('pool', a.sel[0][0:64, :], 1.0, ["a_sel"])
    P.memset('pool', a.sel[1][64:128, :], 1.0, ["a_sel"])
    P.memset('pool', a.v[:, :, :, 128:129], 1.0, [("a_v", i) for i in range(16)])

    for grp_ in range(2):
        P.dma(a.wv, dr["dv_b"][grp_], reads=[("dv_b", grp_)], writes=["a_wv"], key="a_wv")
        for tb in range(16):
            bank = a.rings['s'].next()
            for kc in range(8):
                P.mm(k.ps[bank], k.UT[:, kc, tb * 128:(tb + 1) * 128], a.wv[:, kc, :], kc == 0, kc == 7,
                     [("UT", tb // 4), "a_wv"], [("ps", bank)])
            P.copy('act' if tb % 2 == 0 else 'dve', a.v[:, tb, grp_ * 4:grp_ * 4 + 4, 0:128],
                   k.ps[bank].rearrange("p (a b) -> p a b", a=4), [("ps", bank)], [("a_v", tb)])

    bf7 = k.ps[7].bitcast(BF16)
    for h in range(8):
        ws = a.rings['w'].next()
        P.dma(a.wq[ws], dr["dq_b"][h], reads=[("dq_b", h)], writes=[("a_wq", ws)], key=("a_wq", ws))
        P.dma(a.wk[ws], dr["dk_b"][h], reads=[("dk_b", h)], writes=[("a_wk", ws)], key=("a_wk", ws))
        qs = a.rings['qk'].next()
        qT, kT = a.qT[qs], a.kT[qs]
        for which, (wt, dst, wkey, dkey) in enumerate(((a.wq[ws], qT, "a_wq", "a_qT"), (a.wk[ws], kT, "a_wk", "a_kT"))):
            for tt in range(4):
                tok = slice(tt * 512, (tt + 1) * 512)
                bank = a.rings['s'].next()
                for kc in range(8):
                    P.mm(k.ps[bank], wt[:, kc, :], k.UT[:, kc, tok], kc == 0, kc == 7,
                         [(wkey, ws), ("UT", tt)], [("ps", bank)])
                P.copy('act', dst[:, tok], k.ps[bank], [("ps", bank)], [(dkey, qs)])
                sq_i = a.rings['sq'].next()
                P.actv('act', a.sq[sq_i], k.ps[bank], AF.Square, [("ps", bank)], [("a_sq", sq_i)])
                for cc in range(2):
                    nb_ = 2 + cc
                    P.mm(k.ps[nb_], a.sel[cc], a.sq[sq_i], True, True, ["a_sel", ("a_sq", sq_i)], [("ps", nb_)])
                    P.op('dve', lambda e, o_=a.mx[:, which, cc, tt:tt + 1], i_=k.ps[nb_]:
                         e.reduce_max(out=o_, in_=i_, axis=AX.X), [("ps", nb_)], ["a_mx"])
        P.op('dve', lambda e: e.reduce_max(out=sm[:, C_MQ:C_MQ + 2], in_=a.mx[:, 0], axis=AX.X), ["a_mx"], ["a_small"])
        P.op('dve', lambda e: e.reduce_max(out=sm[:, C_MK:C_MK + 2], in_=a.mx[:, 1], axis=AX.X), ["a_mx"], ["a_small"])
        P.tt('dve', sm[:, C_M:C_M + 2], sm[:, C_MQ:C_MQ + 2], sm[:, C_MK:C_MK + 2], ALU.mult, ["a_small"], ["a_small"])
        P.actv('act', sm[:, C_M:C_M + 2], sm[:, C_M:C_M + 2], AF.Sqrt, ["a_small"], ["a_small"])
        P.ts('dve', sm[:, C_BIAS:C_BIAS + 2], sm[:, C_M:C_M + 2], -0.125, a.cfar[:, h:h + 1], ALU.mult, ALU.add,
             ["a_small"], ["a_small"])
        for g in range(4):
            hst_i = a.rings['hst'].next()
            hst = a.hst[hst_i]
            for cc in range(2):
                prt = slice(cc * 64, (cc + 1) * 64)
                nkb = 4 * g + 4
                pend = None

                def emit_pv(pd):
                    kb_, p_i_, jmin_ = pd
                    for j in range(jmin_, 4):
                        qb = 4 * g + j
                        P.mm(k.ps[2 + j][:, 0:129], a.pT[p_i_][:, j * 128:(j + 1) * 128], a.v[:, kb_, h, 0:129],
                             kb_ == 0, kb_ == qb, [("a_pT", p_i_), ("a_v", kb_)], [("ps", 2 + j)])

                for kb in range(nkb):
                    jmin = max(0, kb - 4 * g)
                    cols = slice(jmin * 128, 512)
                    bank = a.rings['s'].next()
                    P.mm(k.ps[bank][:, cols], kT[prt, kb * 128:(kb + 1) * 128],
                         qT[prt, g * 512 + jmin * 128:(g + 1) * 512], True, True,
                         [("a_kT", qs), ("a_qT", qs)], [("ps", bank)])
                    p_i = a.rings['pT'].next()
                    pT = a.pT[p_i]
                    P.actv('act', pT[:, cols], k.ps[bank][:, cols], AF.Exp, [("ps", bank), "a_small"],
                           [("a_pT", p_i)], bias=sm[:, C_BIAS + cc:C_BIAS + cc + 1], scale=0.125)
                    for j in range(jmin, 4):
                        qb = 4 * g + j
                        if kb == qb or kb == qb - 1:
                            wh = 0 if kb == qb else 1
                            P.tt('dve', pT[:, j * 128:(j + 1) * 128], pT[:, j * 128:(j + 1) * 128],
                                 a.eb[:, h, wh, :], ALU.mult, [("a_pT", p_i), "a_eb"], [("a_pT", p_i)])
                    if pend is not None:
                        emit_pv(pend)
                    pend = (kb, p_i, jmin)
                emit_pv(pend)
                ev_i = a.rings['ev'].next()
                ev = a.ev[ev_i]
                for j in range(4):
                    P.copy('dve', ev[:, j, 0:129], k.ps[2 + j][:, 0:129], [("ps", 2 + j)], [("a_ev", ev_i)])
                for j in range(4):
                    acc = ev[:, j, :]
                    rr = sm[:, C_R + cc:C_R + cc + 1]
                    P.op('dve', lambda e, o_=rr, i_=acc[:, 128:129]: e.reciprocal(out=o_, in_=i_),
                         [("a_ev", ev_i)], ["a_small"])
                    if cc == 0:
                        P.ts('dve', a.t0[:, j, :], acc[:, 0:128], rr, None, ALU.mult, None,
                             [("a_ev", ev_i), "a_small"], [("a_t0", j)])
                    else:
                        P.tt('dve', rr, rr, sm[:, C_NEGLAM:C_NEGLAM + 1], ALU.mult, ["a_small"], ["a_small"])
                        o_i = a.rings['o'].next()
                        o = a.o[o_i]
                        P.stt('dve', o, acc[:, 0:128], rr, a.t0[:, j, :], ALU.mult, ALU.add,
                              [("a_ev", ev_i), "a_small", ("a_t0", j)], [("a_o", o_i)])
                        ss = sm[:, C_SS:C_SS + 1]
                        P.actv('act', a.ob[o_i], o, AF.Square, [("a_o", o_i)], [("a_ob", o_i), "a_small"], accum=ss)
                        P.ts('dve', ss, ss, 1.0 / 128.0, LN_EPS, ALU.mult, ALU.add, ["a_small"], ["a_small"])
                        P.actv('act', ss, ss, AF.Sqrt, ["a_small"], ["a_small"])
                        P.op('dve', lambda e, o_=ss: e.reciprocal(out=o_, in_=o_), ["a_small"], ["a_small"])
                        P.stt('dve', a.ob[o_i], o, ss, a.ng[:, h * 128:(h + 1) * 128], ALU.mult, ALU.mult,
                              [("a_o", o_i), "a_small", "a_ng"], [("a_ob", o_i)])
                        P.tr(bf7[:, j * 128:(j + 1) * 128], a.ob[o_i], c.identb, [("a_ob", o_i), "identb"], [("ps", 7)])
            P.copy('act', hst, bf7[:, 0:512], [("ps", 7)], [("a_hst", hst_i)])
            P.dma(k.htd[:, h, g * 512:(g + 1) * 512], hst, reads=[("a_hst", hst_i)], writes=[("htd", g)],
                  key=("a_hst", hst_i))
    outproj_ln(k, b, sub, "dwout", is_last)


_CACHE = {}


def kernel(**inputs):
    in_maps = prep_inputs(inputs)
    if "nc" not in _CACHE:
        _CACHE["nc"] = build_program()[0]
    nc = _CACHE["nc"]
    res = run_bass_kernel_spmd(nc, in_maps, core_ids=list(range(NCORES)))
    out = np.concatenate([np.asarray(r["out"]) for r in res.results], axis=0)
    return out.astype(np.float32)
```
